# Optimizing a Trainium2 kernel written in Bass

```python
import jax, jax.numpy as jnp
from jax import lax
import numpy as np

D_MODEL = 1024
BATCH = 2
SEQ = 8192
DEPTH = 1

CTX_LEN = 256
GRID_W = 64
HEAD_SIZE = 64
N_HEADS_A = 16
WIDTH_A = N_HEADS_A * HEAD_SIZE
WIDTH_B = D_MODEL
CONV_K = 31
LORA_W = 64
LORA_A = 64
RWKV_COLS = 3 * WIDTH_A + 2 * LORA_W + 2 * LORA_A
IN_COLS = RWKV_COLS + WIDTH_A + 2 * WIDTH_B + WIDTH_B + 2 * D_MODEL
MOD_COLS = 3 * D_MODEL
NORM_EPS = 1e-6
LNX_EPS = 64e-5
LN_EPS = 1e-5

kernel_name = 'bidir_rwkv7_conformer_hybrid_dit'


def rmsnorm(x, w):
    xf = x.astype(jnp.float32)
    y = xf * lax.rsqrt(jnp.mean(xf * xf, axis=-1, keepdims=True) + NORM_EPS)
    return (y * w.astype(jnp.float32)).astype(x.dtype)


def layernorm(x, w, b):
    xf = x.astype(jnp.float32)
    mu = jnp.mean(xf, axis=-1, keepdims=True)
    var = jnp.mean(jnp.square(xf - mu), axis=-1, keepdims=True)
    return ((xf - mu) * lax.rsqrt(var + LN_EPS) * w + b).astype(x.dtype)


def grid_shift(p, rows):
    b, t, ch = p.shape
    q = p.reshape(b, rows, GRID_W, ch // 4, 4)
    left = jnp.pad(q[:, :, :-1, :, 0], ((0, 0), (0, 0), (1, 0), (0, 0)))
    right = jnp.pad(q[:, :, 1:, :, 1], ((0, 0), (0, 0), (0, 1), (0, 0)))
    up = jnp.pad(q[:, :-1, :, :, 2], ((0, 0), (1, 0), (0, 0), (0, 0)))
    down = jnp.pad(q[:, 1:, :, :, 3], ((0, 0), (0, 1), (0, 0), (0, 0)))
    return jnp.stack([left, right, up, down], axis=-1).reshape(b, t, ch)


def seq_shift(p):
    b, t, ch = p.shape
    q = p.reshape(b, t, ch // 2, 2)
    prev = jnp.pad(q[:, :-1, :, 0], ((0, 0), (1, 0), (0, 0)))
    nxt = jnp.pad(q[:, 1:, :, 1], ((0, 0), (0, 1), (0, 0)))
    return jnp.stack([prev, nxt], axis=-1).reshape(b, t, ch)


def rwkv_prepare(mixed, w0, w2, a0, a2, k_k, k_a):
    b, t, _ = mixed.shape
    m = mixed.astype(jnp.float32)
    o1, o2, o3 = WIDTH_A, 2 * WIDTH_A, 3 * WIDTH_A
    r, k, v = m[..., :o1], m[..., o1:o2], m[..., o2:o3]
    low = m[..., o3:]
    wd = (low[..., :LORA_W], low[..., LORA_W:2 * LORA_W])
    ad = (low[..., 2 * LORA_W:2 * LORA_W + LORA_A], low[..., 2 * LORA_W + LORA_A:])
    heads = lambda z: z.reshape(b, t, N_HEADS_A, HEAD_SIZE)
    kk = heads(k * k_k)
    kk = kk * lax.rsqrt(jnp.maximum(jnp.sum(kk * kk, axis=-1, keepdims=True), 1e-24))
    dirs = []
    for d in range(2):
        wlog = -jax.nn.softplus(-(w0[d] + jnp.tanh(wd[d]) @ w2[d])) - 0.5
        decay = jnp.exp(-jnp.exp(wlog))
        a = jax.nn.sigmoid(a0[d] + ad[d] @ a2[d])
        kd = k * (1.0 + (a - 1.0) * k_a)
        dirs.append((heads(decay), heads(kd), kk * heads(a)))
    return heads(r), heads(v), kk, dirs


def wkv_scan(s0, r, w, k, v, a_vec, b_vec, reverse, inclusive):
    def step(s, inp):
        rt, wt, kt, vt, at, bt = inp
        sa = jnp.einsum('bhij,bhj->bhi', s, at)
        s_new = s * wt[:, :, None, :] + sa[..., None] * bt[:, :, None, :] + vt[..., None] * kt[:, :, None, :]
        y = jnp.einsum('bhij,bhj->bhi', s_new if inclusive else s, rt)
        return s_new, y
    xs = tuple(jnp.moveaxis(z, 1, 0) for z in (r, w, k, v, a_vec, b_vec))
    s_fin, ys = lax.scan(step, s0, xs, reverse=reverse)
    return jnp.moveaxis(ys, 0, 1), s_fin


def rwkv_scans(prep, s0_f, s0_b):
    r, v, kk, dirs = prep
    (dec_f, k_f, b_f), (dec_b, k_b, b_b) = dirs
    y_f, s_f = wkv_scan(s0_f, r, dec_f, k_f, v, -kk, b_f, False, True)
    y_b, s_b = wkv_scan(s0_b, r, dec_b, k_b, v, -kk, b_b, True, False)
    return y_f + y_b, s_f, s_b


def rwkv_readout(y_sum, prep, r_k, lnx_w, lnx_b):
    r, v, kk, dirs = prep
    k_f = dirs[0][1]
    b, t = y_sum.shape[:2]
    mu = jnp.mean(y_sum, axis=-1, keepdims=True)
    var = jnp.mean(jnp.square(y_sum - mu), axis=-1, keepdims=True)
    yn = ((y_sum - mu) * lax.rsqrt(var + LNX_EPS)).reshape(b, t, WIDTH_A) * lnx_w + lnx_b
    bonus = jnp.sum(r * k_f * r_k, axis=-1, keepdims=True) * v
    return yn + bonus.reshape(b, t, WIDTH_A)


def conformer_conv(glu_in, conv_w, conv_b, cln_w, cln_b):
    u, g = jnp.split(glu_in, 2, axis=-1)
    z = u * jax.nn.sigmoid(g)
    z = lax.conv_general_dilated(z, conv_w[:, None, :], window_strides=(1,),
                                 padding=[(CONV_K // 2, CONV_K // 2)],
                                 dimension_numbers=('NWC', 'WIO', 'NWC'),
                                 feature_group_count=WIDTH_B) + conv_b
    return jax.nn.silu(layernorm(z, cln_w, cln_b))


def mixer_out(p, y_sum, prep, r_k, lnx_w, lnx_b, conv_w, conv_b, cln_w, cln_b, w_proj_a, w_proj_b, w_out):
    o = RWKV_COLS
    g_a = p[..., o:o + WIDTH_A]; o += WIDTH_A
    glu = p[..., o:o + 2 * WIDTH_B]; o += 2 * WIDTH_B
    g_b = p[..., o:o + WIDTH_B]; o += WIDTH_B
    m_a = p[..., o:o + D_MODEL]
    m_b = p[..., o + D_MODEL:]
    y_a = rwkv_readout(y_sum, prep, r_k, lnx_w, lnx_b).astype(p.dtype) * jax.nn.silu(g_a)
    y_b = conformer_conv(glu, conv_w, conv_b, cln_w, cln_b) * jax.nn.silu(g_b)
    merged = jax.nn.sigmoid(m_a) * (y_a @ w_proj_a) + jax.nn.sigmoid(m_b) * (y_b @ w_proj_b)
    return merged @ w_out


def hybrid_layer(x, xc, c, c_ctx, rows, last, norm_w, w_mod, b_mod, w_in, mu_shift, w0, w2, a0, a2,
                 k_k, k_a, r_k, lnx_w, lnx_b, conv_w, conv_b, cln_w, cln_b, w_proj_a, w_proj_b, w_out):
    shift, scale, gate = jnp.split(jax.nn.silu(c) @ w_mod + b_mod, 3, axis=-1)
    shift_c, scale_c, gate_c = jnp.split(jax.nn.silu(c_ctx) @ w_mod + b_mod, 3, axis=-1)
    h = rmsnorm(x, norm_w) * (1.0 + scale[:, None]) + shift[:, None]
    hc = rmsnorm(xc, norm_w) * (1.0 + scale_c) + shift_c
    p = h @ w_in
    pc = hc @ (w_in[:, :RWKV_COLS] if last else w_in)
    rk = p[..., :RWKV_COLS]
    rk = rk + mu_shift * (grid_shift(rk, rows) - rk)
    rkc = pc[..., :RWKV_COLS]
    rkc = rkc + mu_shift * (seq_shift(rkc) - rkc)
    prep_c = rwkv_prepare(rkc, w0, w2, a0, a2, k_k, k_a)
    prep = rwkv_prepare(rk, w0, w2, a0, a2, k_k, k_a)
    s0 = jnp.zeros((x.shape[0], N_HEADS_A, HEAD_SIZE, HEAD_SIZE), jnp.float32)
    yc_sum, sc_f, sc_b = rwkv_scans(prep_c, s0, s0)
    y_sum, _, _ = rwkv_scans(prep, sc_f, sc_b)
    x_new = x + gate[:, None] * mixer_out(p, y_sum, prep, r_k, lnx_w, lnx_b, conv_w, conv_b,
                                          cln_w, cln_b, w_proj_a, w_proj_b, w_out)
    if last:
        return x_new, xc
    xc_new = xc + gate_c * mixer_out(pc, yc_sum, prep_c, r_k, lnx_w, lnx_b, conv_w, conv_b,
                                     cln_w, cln_b, w_proj_a, w_proj_b, w_out)
    return x_new, xc_new


def setup_inputs(seed: int = 0) -> dict:
    key = jax.random.key(seed)
    ks = jax.random.split(key, 32)
    L, D = DEPTH, D_MODEL
    f32 = jnp.float32
    nrm = lambda k, shape, s: jax.random.normal(k, shape, f32) * s
    return {
        'x': nrm(ks[0], (BATCH, SEQ, D), 1.0),
        'c': nrm(ks[1], (BATCH, D), 1.0),
        'ctx': nrm(ks[2], (BATCH, CTX_LEN, D), 1.0),
        'c_ctx': nrm(ks[3], (D,), 1.0),
        'norm_w': 1.0 + nrm(ks[4], (L, D), 0.02),
        'w_mod': nrm(ks[5], (L, D, MOD_COLS), 0.5 * D ** -0.5),
        'b_mod': nrm(ks[6], (L, MOD_COLS), 0.02),
        'w_in': nrm(ks[7], (L, D, IN_COLS), D ** -0.5),
        'mu_shift': jax.random.uniform(ks[8], (L, RWKV_COLS), f32, 0.0, 1.0),
        'w0': jax.random.uniform(ks[9], (L, 2, WIDTH_A), f32, -6.0, 1.0),
        'w2': nrm(ks[10], (L, 2, LORA_W, WIDTH_A), 0.5 * LORA_W ** -0.5),
        'a0': nrm(ks[11], (L, 2, WIDTH_A), 0.5),
        'a2': nrm(ks[12], (L, 2, LORA_A, WIDTH_A), 0.5 * LORA_A ** -0.5),
        'k_k': 0.85 + nrm(ks[13], (L, WIDTH_A), 0.05),
        'k_a': 1.0 + nrm(ks[14], (L, WIDTH_A), 0.05),
        'r_k': nrm(ks[15], (L, N_HEADS_A, HEAD_SIZE), 0.1),
        'lnx_w': 1.0 + nrm(ks[16], (L, WIDTH_A), 0.02),
        'lnx_b': nrm(ks[17], (L, WIDTH_A), 0.02),
        'conv_w': nrm(ks[18], (L, CONV_K, WIDTH_B), CONV_K ** -0.5),
        'conv_b': nrm(ks[19], (L, WIDTH_B), 0.02),
        'cln_w': 1.0 + nrm(ks[20], (L, WIDTH_B), 0.02),
        'cln_b': nrm(ks[21], (L, WIDTH_B), 0.02),
        'w_proj_a': nrm(ks[22], (L, WIDTH_A, D), WIDTH_A ** -0.5),
        'w_proj_b': nrm(ks[23], (L, WIDTH_B, D), WIDTH_B ** -0.5),
        'w_out': nrm(ks[24], (L, D, D), D ** -0.5),
        'final_norm_w': 1.0 + nrm(ks[25], (D,), 0.02),
    }


def reference(x, c, ctx, c_ctx, norm_w, w_mod, b_mod, w_in, mu_shift, w0, w2, a0, a2, k_k, k_a, r_k,
              lnx_w, lnx_b, conv_w, conv_b, cln_w, cln_b, w_proj_a, w_proj_b, w_out, final_norm_w):
    rows = x.shape[1] // GRID_W
    xc = ctx
    for l in range(DEPTH):
        x, xc = hybrid_layer(x, xc, c, c_ctx, rows, l == DEPTH - 1, norm_w[l], w_mod[l], b_mod[l],
                             w_in[l], mu_shift[l], w0[l], w2[l], a0[l], a2[l], k_k[l], k_a[l], r_k[l],
                             lnx_w[l], lnx_b[l], conv_w[l], conv_b[l], cln_w[l], cln_b[l],
                             w_proj_a[l], w_proj_b[l], w_out[l])
    return rmsnorm(x, final_norm_w)
```

```python
import numpy as np
from contextlib import ExitStack
import concourse.bass as bass
import concourse.mybir as mybir
from concourse.bass_utils import run_bass_kernel_spmd

F32 = mybir.dt.float32
BF16 = mybir.dt.bfloat16
AF = mybir.ActivationFunctionType
ALU = mybir.AluOpType
AX = mybir.AxisListType

SAME_ENG_SYNC = True
USE_F32R = False
N_DMA_SEMS = 16

D = 1024
NHEAD = 16
SEQ = 8192
CTXL = 256
HP = 8
NPASS = NHEAD // HP
CH = HP * 64
NFLAG = 52
NOWN = 16
NSLOT = NFLAG + NOWN
NCOLT = 448
EM05 = float(np.exp(-0.5))


class Prog:
    CENGS = ['pe', 'act', 'dve', 'pool']
    ENGS = ['pe', 'act', 'dve', 'pool', 'sp']

    def __init__(self, nc, stack):
        self.nc = nc
        self.streams = {e: [] for e in self.ENGS}
        self.cnt = {e: 0 for e in self.CENGS}
        self.sems = {}
        for e in self.CENGS:
            self.sems[e] = stack.enter_context(nc.semaphore("s_" + e))
        for i in range(N_DMA_SEMS):
            self.sems[('dma', i)] = stack.enter_context(nc.semaphore("s_dma%d" % i))
        self.dma_cnt = [0] * N_DMA_SEMS
        self.dma_rr = 0
        self.lastw = {}
        self.readers = {}
        self.nops = 0
        self.emitted = {e: 0 for e in self.ENGS}
        self.seen = {e: {} for e in self.ENGS}

    @staticmethod
    def _key(b):
        return b if isinstance(b, str) else b.tensor.name

    def op(self, eng, fn, reads=(), writes=(), dma=False):
        waits = {}

        def need(tok):
            k, v, e = tok
            if e == eng and (eng == 'pe' or not SAME_ENG_SYNC) and not dma:
                return
            if waits.get(k, 0) < v:
                waits[k] = v
        rk = [self._key(b) for b in reads]
        wk = [self._key(b) for b in writes]
        for b in rk:
            if b in self.lastw:
                need(self.lastw[b])
        for b in wk:
            if b in self.lastw:
                need(self.lastw[b])
            for t in self.readers.get(b, {}).values():
                need(t)
        if dma:
            i = self.dma_rr
            self.dma_rr = (i + 1) % N_DMA_SEMS
            prev = self.dma_cnt[i]
            if prev:
                waits[('dma', i)] = max(waits.get(('dma', i), 0), 16 * prev)
            self.dma_cnt[i] += 1
            tok = (('dma', i), 16 * self.dma_cnt[i], 'dma')
            inc = (('dma', i), 16)
        else:
            self.cnt[eng] += 1
            tok = (eng, self.cnt[eng], eng)
            inc = (eng, 1)
        self.streams[eng].append((waits, fn, inc))
        self.nops += 1
        for b in rk:
            d = self.readers.setdefault(b, {})
            old = d.get(tok[0])
            if old is None or old[1] < tok[1]:
                d[tok[0]] = tok
        for b in wk:
            self.lastw[b] = tok
            self.readers[b] = {}
        return tok

    def finish(self):
        waits = {}
        for i in range(N_DMA_SEMS):
            if self.dma_cnt[i]:
                waits[('dma', i)] = 16 * self.dma_cnt[i]
        for e in self.CENGS:
            if self.cnt[e]:
                waits[e] = self.cnt[e]
        for e in self.ENGS:
            self.streams[e].append((dict(waits), None, None))

    def emit(self):
        sems = self.sems
        streams = self.streams
        emitted = self.emitted
        seen_all = self.seen
        with self.nc.Block() as block:
            def run(name):
                def body(eng):
                    seen = seen_all[name]
                    lst = streams[name]
                    for waits, fn, inc in lst[emitted[name]:]:
                        for k, v in waits.items():
                            if seen.get(k, 0) >= v:
                                continue
                            seen[k] = v
                            eng.wait_ge(sems[k], v)
                        if fn is None:
                            continue
                        ins = fn(eng)
                        ins.then_inc(sems[inc[0]], inc[1])
                    emitted[name] = len(lst)
                return body
            block.tensor(run('pe'))
            block.scalar(run('act'))
            block.vector(run('dve'))
            block.gpsimd(run('pool'))
            block.sync(run('sp'))

    def dma(self, out, in_, reads=None, writes=None, eng='sp'):
        return self.op(eng, lambda e: e.dma_start(out=out, in_=in_),
                       reads if reads is not None else [in_],
                       writes if writes is not None else [out], dma=True)

    def mm(self, out, lhsT, rhs, start=True, stop=True, r32=False):
        if r32 and USE_F32R:
            l2, r2 = lhsT.bitcast(mybir.dt.float32r), rhs.bitcast(mybir.dt.float32r)
            return self.op('pe', lambda e: e.matmul(out, l2, r2, start=start, stop=stop), [lhsT, rhs], [out])
        return self.op('pe', lambda e: e.matmul(out, lhsT, rhs, start=start, stop=stop),
                       [lhsT, rhs], [out])

    def tr(self, out, in_, ident):
        return self.op('pe', lambda e: e.transpose(out, in_, ident), [in_, ident], [out])

    def act(self, out, in_, func, scale=1.0, bias=0.0, accum_out=None):
        rd = [in_]
        wr = [out]
        if not isinstance(scale, (int, float)):
            rd.append(scale)
        if not isinstance(bias, (int, float)):
            rd.append(bias)
        if accum_out is not None:
            wr.append(accum_out)
            f = lambda e: e.activation(out=out, in_=in_, func=func, scale=scale, bias=bias, accum_out=accum_out)
        else:
            f = lambda e: e.activation(out=out, in_=in_, func=func, scale=scale, bias=bias)
        return self.op('act', f, rd, wr)

    def tt(self, eng, out, in0, in1, op):
        return self.op(eng, lambda e: e.tensor_tensor(out=out, in0=in0, in1=in1, op=op), [in0, in1], [out])

    def ts(self, eng, out, in0, s1, op0, s2=None, op1=None):
        rd = [in0]
        if not isinstance(s1, (int, float)):
            rd.append(s1)
        if s2 is not None and not isinstance(s2, (int, float)):
            rd.append(s2)
        if op1 is None and eng == 'pool' and op0 == ALU.mult:
            f = lambda e: e.tensor_scalar(out=out, in0=in0, scalar1=s1, scalar2=0.0, op0=ALU.mult, op1=ALU.add)
        elif op1 is None and eng == 'pool' and op0 == ALU.add:
            f = lambda e: e.tensor_scalar(out=out, in0=in0, scalar1=s1, scalar2=1.0, op0=ALU.add, op1=ALU.mult)
        elif op1 is None:
            f = lambda e: e.tensor_scalar(out=out, in0=in0, scalar1=s1, scalar2=None, op0=op0)
        else:
            f = lambda e: e.tensor_scalar(out=out, in0=in0, scalar1=s1, scalar2=s2, op0=op0, op1=op1)
        return self.op(eng, f, rd, [out])

    def stt(self, out, in0, scalar, in1, op0, op1):
        rd = [in0, in1]
        if not isinstance(scalar, (int, float)):
            rd.append(scalar)
        return self.op('dve', lambda e: e.scalar_tensor_tensor(out=out, in0=in0, scalar=scalar, in1=in1, op0=op0, op1=op1),
                       rd, [out])

    def copy(self, eng, out, in_):
        if eng == 'act':
            return self.act(out, in_, AF.Copy)
        return self.op(eng, lambda e: e.tensor_copy(out=out, in_=in_), [in_], [out])

    def memset(self, eng, ap, val):
        return self.op(eng, lambda e: e.memset(ap, val), [], [ap])

    def recip(self, out, in_):
        return self.op('dve', lambda e: e.reciprocal(out=out, in_=in_), [in_], [out])

    def reduce(self, out, in_, op=ALU.add):
        return self.op('dve', lambda e: e.tensor_reduce(out=out, in_=in_, axis=AX.X, op=op), [in_], [out])

    def aselect(self, ap, cmp, fill, pattern, cm, base=0):
        return self.op('pool', lambda e: e.affine_select(out=ap, in_=ap, compare_op=cmp, fill=fill, base=base,
                                                         pattern=pattern, channel_multiplier=cm), [ap], [ap])


LIM = {'stop': None, 'nflag': NFLAG, 'nown': NOWN, 'passB': True, 'npass': NPASS, 'sub': 99}


def build(debug=()):
    nc = bass.Bass("TRN2", target_bir_lowering=False)
    dbg_specs = {}

    def din(name, shape, dt=F32):
        return nc.dram_tensor(name, list(shape), dt, kind="ExternalInput").ap()

    xs = din("xs", [NSLOT, 128, 8, 256])
    valid = din("valid", [NSLOT, 128, 256])
    fl = din("fl", [128, NFLAG, 3])
    x2T = din("x2T", [128, 8, 2080])
    edge = din("edge", [128, 2])
    x_own = din("x_own", [NOWN, 128, D])
    c2 = din("c2", [128, 8, 2])
    w_mod = din("w_mod", [128, 8, 3072])
    b_modT = din("b_modT", [128, 24])
    bgate = din("bgate", [128, D])
    norm_wT = din("norm_wT", [128, 8])
    wrk = din("wrk", [NPASS, 128, 8, 4, NCOLT])
    mu_rk = din("mu_rk", [NPASS, 128, 4, NCOLT])
    wga = din("wga", [NPASS, 128, 8, CH])
    kkrow_d = din("kkrow", [NPASS, 128, CH])
    karow_d = din("karow", [NPASS, 128, CH])
    rkrow_d = din("rkrow", [NPASS, 128, CH])
    lnxw_d = din("lnxw", [NPASS, 128, CH])
    lnxb_d = din("lnxb", [NPASS, 128, CH])
    w2s_d = din("w2s", [NPASS, 128, CH])
    a2s_d = din("a2s", [NPASS, 128, CH])
    w0s_d = din("w0s", [NPASS, 33, CH])
    a0s_d = din("a0s", [NPASS, 33, CH])
    wblk = din("wblk", [16, 128, 8, 512])
    cwT = din("cwT", [128, 8, 31])
    cvec = din("cvec", [128, 3, 8])
    fnw = din("fnw", [128, D])
    y_out = nc.dram_tensor("y_out", [NOWN, 128, D], F32, kind="ExternalOutput").ap()
    scr_pqg = nc.dram_tensor("scr_pqg", [NOWN, 2, 64, 4, 256], F32, kind="Internal").ap()
    scr_y = nc.dram_tensor("scr_y", [NOWN, 128, 3, CH], F32, kind="Internal").ap()
    scr_ya = nc.dram_tensor("scr_ya", [128, 8, 2048], BF16, kind="Internal").ap()

    with ExitStack() as top:
        P = Prog(nc, top)

        uid = [0]

        def sbuf(st, name, shape, dt=F32):
            uid[0] += 1
            return st.enter_context(nc.sbuf_tensor("%s_u%d" % (name, uid[0]), list(shape), dt))

        dbg_out = {}

        def dbg(name, ap, shape, dt=F32):
            if name not in debug:
                return
            o = nc.dram_tensor("dbg_" + name, list(shape), dt, kind="ExternalOutput").ap()
            dbg_out[name] = o
            P.dma(o, ap)

        banks = [top.enter_context(nc.psum_tensor("ps%d" % i, [128, 512], F32)) for i in range(8)]
        psb = None
        bank_rr = [0]

        def bank():
            b = banks[bank_rr[0] % 8]
            bank_rr[0] += 1
            return b

        ident = sbuf(top, "ident", [128, 128])
        identb = sbuf(top, "identb", [128, 128], BF16)
        ones32 = sbuf(top, "ones32", [128, 128])
        onesb = sbuf(top, "onesb", [128, 128], BF16)
        gsh = sbuf(top, "gsh", [128, 8, 4])
        gateB = sbuf(top, "gateB", [128, D])

        P.memset('pool', ident[:], 0.0)
        P.aselect(ident[:], ALU.not_equal, 1.0, [[-1, 128]], 1)
        P.copy('pool', identb[:], ident[:])
        P.memset('pool', ones32[:], 1.0)
        P.memset('pool', onesb[:], 1.0)

        with ExitStack() as st:
            wm = sbuf(st, "wm", [128, 8, 3072])
            c2s = sbuf(st, "c2s", [128, 8, 2])
            sc = sbuf(st, "sc", [128, 8, 2])
            scB = sbuf(st, "scB", [128, 8, 128])
            bm = sbuf(st, "bm", [128, 24])
            nw = sbuf(st, "nw", [128, 8])
            modT = sbuf(st, "modT", [128, 24, 2])
            bg = sbuf(st, "bg", [128, D])
            for k in range(8):
                P.dma(wm[:, k, :], w_mod[:, k, :])
            P.dma(c2s[:], c2)
            P.dma(bm[:], b_modT)
            P.dma(nw[:], norm_wT)
            P.dma(bg[:], bgate)
            P.act(sc[:], c2s[:], AF.Exp, scale=-1.0)
            P.ts('dve', sc[:], sc[:], 1.0, ALU.add)
            P.recip(sc[:], sc[:])
            P.tt('dve', sc[:], sc[:], c2s[:], ALU.mult)
            pm = bank()
            for j in (range(24) if LIM['sub'] >= 2 else []):
                for k in range(8):
                    P.mm(pm[:, 2 * j:2 * j + 2], wm[:, k, 128 * j:128 * j + 128], sc[:, k, :],
                         start=(k == 0), stop=(k == 7))
            if LIM['sub'] >= 3:
                P.tt('dve', modT[:], pm[:, 0:48].rearrange("p (j t) -> p j t", t=2),
                     bm[:].unsqueeze(2).to_broadcast([128, 24, 2]), ALU.add)
            for t_, (go, so) in (enumerate([(0, 1), (2, 3)]) if LIM['sub'] >= 4 else []):
                P.ts('dve', gsh[:, :, go], modT[:, 8:16, t_], 1.0, ALU.add)
                P.tt('dve', gsh[:, :, go], gsh[:, :, go], nw[:], ALU.mult)
                P.copy('dve', gsh[:, :, so], modT[:, 0:8, t_])
            if LIM['sub'] >= 5:
                P.copy('dve', scB[:], sc[:, :, 0:1].to_broadcast([128, 8, 128]))
            for n in (range(2) if LIM['sub'] >= 6 else []):
                pg = bank()
                for k in range(8):
                    P.mm(pg[:, :], scB[:, k, :], wm[:, k, 2048 + 512 * n:2048 + 512 * n + 512],
                         start=(k == 0), stop=(k == 7))
                P.tt('dve', gateB[:, 512 * n:512 * n + 512], pg[:, :], bg[:, 512 * n:512 * n + 512], ALU.add)
            dbg("sc", sc[:], [128, 8, 2])
            dbg("gsh", gsh[:], [128, 8, 4])
            dbg("gateB", gateB[:], [128, D])
            if LIM['stop'] == 'p0':
                P.finish()
            P.emit()
        if LIM['stop'] == 'p0':
            return nc, list(dbg_out.keys())

        for hp in range(LIM['npass']):
            with ExitStack() as st:
                phase1(nc, P, st, sbuf, bank, psb, hp, locals(), dbg)
                if LIM['stop'] == 'p1' and hp == LIM['npass'] - 1:
                    P.finish()
                P.emit()
        if LIM['stop'] == 'p1':
            return nc, list(dbg_out.keys())

        with ExitStack() as st:
            phase2(nc, P, st, sbuf, bank, psb, locals(), dbg)
            P.finish()
            P.emit()
    return nc, list(dbg_out.keys())


PACE = {'f1': 2, 'f2': 2}


def run_tasks(tasks):
    active = []
    for t in tasks:
        if t is None:
            continue
        if isinstance(t, tuple):
            if t[0] is not None:
                active.append([t[0], t[1]])
        else:
            active.append([t, 1])
    rnd = 0
    while active:
        only_slow = all(st_ > 1 for _, st_ in active)
        for item in list(active):
            g_, st_ = item
            if st_ > 1 and not only_slow and rnd % st_ != 0:
                continue
            try:
                next(g_)
            except StopIteration:
                active.remove(item)
        rnd += 1


def phase1(nc, P, st, sbuf, bank, psb, hp, G, dbg):
    xs, valid, fl = G['xs'], G['valid'], G['fl']
    ident, identb, ones32, onesb, gsh = G['ident'], G['identb'], G['ones32'], G['onesb'], G['gsh']
    scr_pqg, scr_y, scr_ya = G['scr_pqg'], G['scr_y'], G['scr_ya']
    banks = G['banks']

    def mk_bank(lst):
        idx = [0]

        def f():
            b_ = lst[idx[0] % len(lst)]
            idx[0] += 1
            return b_
        return f
    bank_f1 = mk_bank(banks[0:2])
    bank_f = mk_bank(banks[2:4])
    bank_g = [mk_bank(banks[4:6]), mk_bank(banks[6:8])]

    W = sbuf(st, "W", [128, 8, 4, NCOLT], BF16)
    Wg = sbuf(st, "Wg", [128, 8, CH], BF16)
    mu = sbuf(st, "mu", [128, 4, NCOLT])
    kkrow = sbuf(st, "kkrow", [128, CH])
    karow = sbuf(st, "karow", [128, CH])
    rkrow = sbuf(st, "rkrow", [128, CH])
    w2s = sbuf(st, "w2s", [128, CH])
    a2s = sbuf(st, "a2s", [128, CH])
    w0s = sbuf(st, "w0s", [33, CH])
    a0s = sbuf(st, "a0s", [33, CH])
    ones33 = sbuf(st, "ones33", [33, 128])
    fls = sbuf(st, "fls", [128, NFLAG, 3])
    xb = sbuf(st, "xb", [128, 8, 256])
    xbf = xb[:].rearrange("p k t -> p (k t)")
    Wst = xbf[:, 0:4 * NCOLT].rearrange("p (x n) -> p x n", x=4)
    for k in range(8):
        P.dma(Wst, G['wrk'][hp, :, k, :, :])
        P.copy('pool', W[:, k, :, :], Wst)
    for k in range(8):
        P.dma(xbf[:, 0:CH], G['wga'][hp, :, k, :])
        P.copy('pool', Wg[:, k, :], xbf[:, 0:CH])
    for dst, src in [(mu, G['mu_rk']), (kkrow, G['kkrow_d']), (karow, G['karow_d']), (rkrow, G['rkrow_d']),
                     (w2s, G['w2s_d']), (a2s, G['a2s_d']), (w0s, G['w0s_d']), (a0s, G['a0s_d'])]:
        P.dma(dst[:], src[hp])
    P.dma(fls[:], fl)
    P.memset('pool', ones33[:], 1.0)
    sel33 = [sbuf(st, "sel33_%d" % d, [33, 128]) for d in range(2)]
    for d in range(2):
        P.memset('pool', sel33[d][:], 0.0)
        P.memset('pool', sel33[d][32 * d:32 * d + 1, :], 1.0)

    Tri = sbuf(st, "Tri", [128, 128])
    TriT = sbuf(st, "TriT", [128, 128])
    SU = sbuf(st, "SU", [128, 128])
    SL = sbuf(st, "SL", [128, 128])
    D1 = sbuf(st, "D1", [128, 128])
    D2 = sbuf(st, "D2", [128, 128])
    mL = sbuf(st, "mL", [128, 128], BF16)
    mR = sbuf(st, "mR", [128, 128], BF16)
    ones64 = sbuf(st, "ones64", [64, 4, 64])
    for t_, cmp, pat, cm in [(Tri, ALU.is_ge, [[1, 128]], -1), (TriT, ALU.is_ge, [[-1, 128]], 1),
                             (SU, ALU.is_gt, [[1, 128]], -1), (SL, ALU.is_gt, [[-1, 128]], 1)]:
        P.memset('pool', t_[:], 1.0)
        P.aselect(t_[:], cmp, 0.0, pat, cm)
    P.tt('pool', D1[:], SL[:], SU[:], ALU.subtract)
    P.tt('pool', D2[:], TriT[:], Tri[:], ALU.subtract)
    P.memset('pool', mL[:], 1.0)
    P.memset('pool', mR[:], 1.0)
    for c0 in (0, 64):
        P.memset('pool', mL[:, c0:c0 + 1], 0.0)
        P.memset('pool', mR[:, c0 + 63:c0 + 64], 0.0)
    P.memset('pool', ones64[:], 1.0)

    vb = sbuf(st, "vb", [128, 256])
    sq = sbuf(st, "sq", [128, 8, 256], BF16)
    rs = sbuf(st, "rs", [128, 256])
    hT = sbuf(st, "hT", [128, 8, 256], BF16)
    hsh1 = sbuf(st, "hsh", [128, 8, 128], BF16)
    hsh = [hsh1, hsh1]
    hd = [sbuf(st, "hd%d" % i, [128, 8, 128], BF16) for i in range(4)]
    pnb = [sbuf(st, "pn%d" % i, [128, 4 * NCOLT]) for i in range(2)]
    mtmp = sbuf(st, "mtmp", [128, NCOLT])
    T = [sbuf(st, "T%d" % i, [128, CH]) for i in range(9)]
    small = sbuf(st, "small", [128, 4, 8])
    lsel = sbuf(st, "lsel", [128, 2, 64])
    ltmp = sbuf(st, "ltmp", [128, 2, 64])
    LW = sbuf(st, "LW", [128, 128])
    LA = sbuf(st, "LA", [128, 128])
    LWT = sbuf(st, "LWT", [128, 128])
    LAT = sbuf(st, "LAT", [128, 128])
    frow = sbuf(st, "frow", [33, 128])
    Trisel = sbuf(st, "Trisel", [128, 128])
    one11 = sbuf(st, "one11", [1, 1])
    P.memset('pool', one11[:], 1.0)
    names = ["Bt", "Kt", "Rt", "Bh", "Kh", "Vb"]
    PO = [{n: sbuf(st, "%s%d" % (n, i), [128, CH], BF16) for n in names} for i in range(2)]
    Zall = [[sbuf(st, "Z%d_%d" % (i, g), [128, 4, 128], BF16) for g in range(2)] for i in range(2)]
    gam = [sbuf(st, "gam%d" % i, [64, HP]) for i in range(2)]
    Msel = [sbuf(st, "Msel%d" % i, [128, 128]) for i in range(2)]
    MselT = [sbuf(st, "MselT%d" % i, [128, 128]) for i in range(2)]
    fbm = [sbuf(st, "fbm%d" % i, [64, 4, 2, 64]) for i in range(2)]
    ypart = [sbuf(st, "ypart%d" % i, [128, 3, CH]) for i in range(2)]
    featT = [sbuf(st, "featT%d" % g, [64, 4, 4, 128], BF16) for g in range(2)]
    Mn = [{n: sbuf(st, "%s_%d" % (n, g), [128, 4, 128], BF16) for n in ["Mab", "MabT", "Mak", "Mrb", "Mrk"]}
          for g in range(2)]
    S = [sbuf(st, "S%d" % g, [64, 4, 2, 64]) for g in range(2)]
    PQG = [sbuf(st, "PQG%d" % g, [64, 4, 256]) for g in range(2)]
    rt1 = [sbuf(st, "rt1_%d" % g, [64, 4, 2, 64]) for g in range(2)]
    yab = sbuf(st, "yab", [128, CH], BF16)
    yat = sbuf(st, "yat", [128, 4, 128], BF16)
    for g in range(2):
        P.memset('pool', S[g][:], 0.0)

    EXP = AF.Exp
    class PV:
        def __init__(self, pn):
            self.pn = pn
            self.kk = pn[:, 0:CH]
            self.v = pn[:, CH:2 * CH]
            self.low = pn[:, 2 * CH:2 * CH + 256]
            self.r = pn[:, 2 * CH + 256:3 * CH + 256]
    pv = [PV(pnb[0]), PV(pnb[1])]
    h3 = lambda ap: ap.rearrange("p (h i) -> p h i", i=64)

    def sigmoid_from(out, in_, scale=-1.0):
        P.act(out, in_, EXP, scale=scale)
        P.act(out, out, AF.Ln, bias=ones32[:, 0:1])
        P.act(out, out, EXP, scale=-1.0)

    def make_h(si, is_ctx):
        gi, so = (2, 3) if is_ctx else (0, 1)
        P.dma(xb[:], xs[si])
        P.dma(vb[:], valid[si])
        P.tt('pool', sq[:], xb[:], xb[:], ALU.mult)
        pstat = bank_f1()
        for k in range(8):
            P.mm(pstat[:, 0:256], onesb[:], sq[:, k, :], start=(k == 0), stop=(k == 7))
        yield
        P.ts('dve', rs[:], pstat[:, 0:256], 1.0 / D, ALU.mult, 1e-6, ALU.add)
        P.act(rs[:], rs[:], AF.Ln)
        P.act(rs[:], rs[:], EXP, scale=-0.5)
        P.tt('pool', xb[:], xb[:], rs[:].unsqueeze(1).to_broadcast([128, 8, 256]), ALU.mult)
        yield
        for k in range(8):
            P.ts('pool', hT[:, k, :], xb[:, k, :], gsh[:, k, gi:gi + 1], ALU.mult, gsh[:, k, so:so + 1], ALU.add)
        for a_ in (0, 192):
            P.tt('pool', hT[:, :, a_:a_ + 64], hT[:, :, a_:a_ + 64],
                 vb[:, a_:a_ + 64].unsqueeze(1).to_broadcast([128, 8, 64]), ALU.mult)
        yield
        for X in range(4):
            if is_ctx:
                off = -1 if X % 2 == 0 else 1
                msk = None
            else:
                off = [-1, 1, -64, 64][X]
                msk = [mL, mR, None, None][X]
            src = hT[:, :, 64 + off:192 + off]
            eng_ = 'pool'
            P.tt(eng_, hd[X][:], src, hT[:, :, 64:192], ALU.subtract)
            if msk is not None:
                for t_ in ((0, 64) if X == 0 else (63, 127)):
                    P.ts(eng_, hd[X][:, :, t_:t_ + 1], hT[:, :, 64 + t_:65 + t_], -1.0, ALU.mult)
        yield

    def big_mm(ncols, pvx):
        pnv = pvx.pn[:].rearrange("p (m x) -> p m x", x=4)
        for X in range(4):
            pa = bank_f1()
            pb_ = bank_f1()
            for k in range(8):
                P.mm(pa[:, 0:ncols], hT[:, k, 64:192], W[:, k, X, 0:ncols], start=(k == 0), stop=(k == 7))
            for k in range(8):
                P.mm(pb_[:, 0:ncols], hd[X][:, k, :], W[:, k, X, 0:ncols], start=(k == 0), stop=(k == 7))
            P.tt('dve', mtmp[:, 0:ncols], pb_[:, 0:ncols], mu[:, X, 0:ncols], ALU.mult)
            P.tt('dve', pnv[:, 0:ncols, X], mtmp[:, 0:ncols], pa[:, 0:ncols], ALU.add)
            yield

    def prep_common(po, pvx):
        kk_, v_ = pvx.kk, pvx.v
        P.tt('pool', T[0][:], kk_, kkrow[:], ALU.mult)
        P.tt('pool', T[1][:], T[0][:], T[0][:], ALU.mult)
        P.reduce(small[:, 0, :], h3(T[1][:]))
        P.ts('dve', small[:, 0, :], small[:, 0, :], 1e-24, ALU.max)
        P.act(small[:, 1, :], small[:, 0, :], AF.Sqrt)
        P.recip(small[:, 1, :], small[:, 1, :])
        P.tt('pool', h3(T[0][:]), h3(T[0][:]), small[:, 1, :].unsqueeze(2).to_broadcast([128, HP, 64]), ALU.mult)
        P.copy('act', po["Vb"][:], v_)
        yield

    def lora(mode, slot, pvx):
        low_ = pvx.low
        lowv = low_.rearrange("p (a d l) -> p a d l", a=2, d=2)
        if mode == 'flag':
            f0 = fls[:, slot, 1:2]
            f1 = fls[:, slot, 0:1]
            P.tt('pool', ltmp[:], lowv[:, :, 1, :], lowv[:, :, 0, :], ALU.subtract)
            P.stt(lsel[:], ltmp[:], f1, lowv[:, :, 0, :], ALU.mult, ALU.add)
            wsel = lsel[:, 0, :]
            asel = lsel[:, 1, :]
            P.act(ltmp[:, 0, :], wsel, EXP, scale=-2.0)
            P.ts('dve', ltmp[:, 0, :], ltmp[:, 0, :], 1.0, ALU.add)
            P.recip(ltmp[:, 0, :], ltmp[:, 0, :])
            P.ts('dve', ltmp[:, 0, :], ltmp[:, 0, :], 2.0, ALU.mult, -1.0, ALU.add)
            P.ts('pool', LW[:, 0:64], ltmp[:, 0, :], f0, ALU.mult)
            P.ts('pool', LW[:, 64:128], ltmp[:, 0, :], f1, ALU.mult)
            P.ts('pool', LA[:, 0:64], asel, f0, ALU.mult)
            P.ts('pool', LA[:, 64:128], asel, f1, ALU.mult)
            P.ts('pool', frow[:], ones33[:], fls[0:33, slot, 2:3], ALU.mult)
        else:
            d = slot
            cs = slice(64 * d, 64 * d + 64)
            P.memset('pool', LW[:], 0.0)
            P.memset('pool', LA[:], 0.0)
            P.act(LW[:, cs], low_[:, cs], EXP, scale=-2.0)
            P.ts('dve', LW[:, cs], LW[:, cs], 1.0, ALU.add)
            P.recip(LW[:, cs], LW[:, cs])
            P.ts('dve', LW[:, cs], LW[:, cs], 2.0, ALU.mult, -1.0, ALU.add)
            P.copy('pool', LA[:, cs], low_[:, 128 + 64 * d:128 + 64 * d + 64])
        pt = bank_f()
        P.tr(pt[:, 0:128], LW[:], ident[:])
        P.tr(pt[:, 128:256], LA[:], ident[:])
        P.copy('act', LWT[:], pt[:, 0:128])
        P.copy('act', LAT[:], pt[:, 128:256])
        fr = frow if mode == 'flag' else sel33[slot]
        pw = bank_f()
        pa = bank_f()
        P.mm(pw[:, 0:CH], LWT[:], w2s[:], start=True, stop=False, r32=True)
        P.mm(pw[:, 0:CH], fr[:], w0s[:], start=False, stop=True)
        P.mm(pa[:, 0:CH], LAT[:], a2s[:], start=True, stop=False, r32=True)
        P.mm(pa[:, 0:CH], fr[:], a0s[:], start=False, stop=True)
        return pw, pa

    def prep_dir(po, zs, gm, pw, pa, tri, readout, bwd, pvx):
        kk_, r_ = pvx.kk, pvx.r
        ld, asg, kd, bv = T[2], T[3], T[4], T[5]
        sigmoid_from(ld[:], pw[:, 0:CH])
        P.act(ld[:], ld[:], AF.Copy, scale=-EM05)
        sigmoid_from(asg[:], pa[:, 0:CH])
        yield
        P.stt(kd[:], asg[:], -1.0, karow[:], ALU.add, ALU.mult)
        P.stt(kd[:], kd[:], 1.0, kk_, ALU.add, ALU.mult)
        P.tt('pool', bv[:], T[0][:], asg[:], ALU.mult)
        pc = bank_f()
        ptot = bank_f()
        P.mm(pc[:, 0:CH], tri, ld[:], r32=True)
        P.mm(ptot[:, 0:CH], ones32[:], ld[:], r32=True)
        yield
        eexc, eninc, etot, ehat = T[6], T[7], T[8], T[3]
        P.tt('dve', eexc[:], pc[:, 0:CH], ld[:], ALU.subtract)
        P.act(eexc[:], eexc[:], EXP)
        P.act(eninc[:], pc[:, 0:CH], EXP, scale=-1.0)
        P.act(etot[:], ptot[:, 0:CH], EXP)
        if readout and not bwd:
            P.act(T[1][:], pc[:, 0:CH], EXP)
        yield
        P.tt('pool', ehat[:], etot[:], eninc[:], ALU.mult)
        for g in range(2):
            P.stt(zs[g][:, :, 0:64], h3(T[0][:, 256 * g:256 * g + 256]), -1.0, h3(eexc[:, 256 * g:256 * g + 256]),
                  ALU.mult, ALU.mult)
        P.tt('pool', po["Bt"][:], bv[:], eninc[:], ALU.mult)
        P.tt('pool', po["Kt"][:], kd[:], eninc[:], ALU.mult)
        yield
        P.tt('pool', po["Bh"][:], bv[:], ehat[:], ALU.mult)
        P.tt('pool', po["Kh"][:], kd[:], ehat[:], ALU.mult)
        if readout:
            P.tt('dve', po["Rt"][:], r_, (eexc if bwd else T[1])[:], ALU.mult)
        pg = bank_f()
        for h in range(HP):
            P.mm(pg[0:64, h:h + 1], etot[0:1, 64 * h:64 * h + 64], one11[:, :])
        P.copy('act', gm[:], pg[0:64, 0:HP])
        yield

    def chunk_group(po, zs, gm, g, mM, mMT, mR_, readout):
        bk = bank_g[g]
        Z = zs[g]
        M = Mn[g]
        ft = featT[g]
        hc = lambda h: slice(256 * g + 64 * h, 256 * g + 64 * h + 64)
        quants = ["At", "Rt", "Bt", "Kt"] if readout else ["At", None, "Bt", "Kt"]
        for pl in range(2):
            psb = bk()[:, :].bitcast(BF16)
            for hh in range(2):
                h = 2 * pl + hh
                for qi, qn in enumerate(quants):
                    if qn is None:
                        continue
                    src_ = Z[:, h, 0:64] if qn == "At" else po[qn][:, hc(h)]
                    P.tr(psb[0:64, (hh * 4 + qi) * 128:(hh * 4 + qi) * 128 + 128], src_, identb[:])
            src = psb[0:64, :].rearrange("p (h q t) -> p h q t", h=2, q=4)
            if readout:
                P.copy('act', ft[:, 2 * pl:2 * pl + 2, :, :], src)
            else:
                P.copy('act', ft[:, 2 * pl:2 * pl + 2, 0, :], src[:, :, 0, :])
                P.copy('act', ft[:, 2 * pl:2 * pl + 2, 2:4, :], src[:, :, 2:4, :])
            yield

        def fa(h, qi):
            return ft[:, h, qi, :]
        v3 = lambda b_: b_[:, :].rearrange("p (h t) -> p h t", t=128)
        bc = lambda m_: m_.unsqueeze(1).to_broadcast([128, 4, 128])
        g1, g2 = bk(), bk()
        for h in range(4):
            P.mm(g1[:, 128 * h:128 * h + 128], fa(h, 2), fa(h, 0))
            P.mm(g2[:, 128 * h:128 * h + 128], fa(h, 0), fa(h, 2))
        P.tt('dve', M["Mab"][:], v3(g1), bc(mM), ALU.mult)
        P.tt('dve', M["MabT"][:], v3(g2), bc(mMT), ALU.mult)
        yield
        g3 = bk()
        for h in range(4):
            P.mm(g3[:, 128 * h:128 * h + 128], fa(h, 3), fa(h, 0))
        P.tt('dve', M["Mak"][:], v3(g3), bc(mM), ALU.mult)
        if readout:
            g4 = bk()
            for h in range(4):
                P.mm(g4[:, 128 * h:128 * h + 128], fa(h, 2), fa(h, 1))
            P.tt('dve', M["Mrb"][:], v3(g4), bc(mR_), ALU.mult)
            yield
            g5 = bk()
            for h in range(4):
                P.mm(g5[:, 128 * h:128 * h + 128], fa(h, 3), fa(h, 1))
            P.tt('dve', M["Mrk"][:], v3(g5), bc(mR_), ALU.mult)
        yield
        pe_ = bk()
        for h in range(4):
            P.mm(pe_[:, 64 * h:64 * h + 64], M["Mak"][:, h, :], po["Vb"][:, hc(h)])
        P.copy('act', Z[:, :, 64:128], pe_[:, 0:256].rearrange("p (h i) -> p h i", i=64))
        yield
        N_, NT_ = M["Mab"], M["MabT"]
        for lvl in range(7):
            pz = bk()
            for h in range(4):
                P.mm(pz[:, 128 * h:128 * h + 128], N_[:, h, :], Z[:, h, :])
            if lvl < 6:
                pn2 = bk()
                for h in range(4):
                    P.mm(pn2[:, 128 * h:128 * h + 128], NT_[:, h, :], N_[:, h, :])
            P.tt('dve', Z[:], v3(pz), Z[:], ALU.add)
            if lvl < 5:
                yield
                pn2t = bk()
                for h in range(4):
                    P.mm(pn2t[:, 128 * h:128 * h + 128], N_[:, h, :], NT_[:, h, :])
            if lvl < 6:
                P.copy('act', N_[:], v3(pn2))
            if lvl < 5:
                P.copy('act', NT_[:], v3(pn2t))
            yield
        pp = bk()
        for h in range(4):
            P.mm(pp[0:64, 64 * h:64 * h + 64], Z[:, h, 0:64], po["Bh"][:, hc(h)])
        for h in range(4):
            P.stt(PQG[g][:, h, 0:64], ident[0:64, 0:64], gm[:, 4 * g + h:4 * g + h + 1], pp[0:64, 64 * h:64 * h + 64],
                  ALU.mult, ALU.add)
        yield
        pq = bk()
        for h in range(4):
            P.mm(pq[0:64, 64 * h:64 * h + 64], po["Bh"][:, hc(h)], Z[:, h, 64:128], start=True, stop=False)
            P.mm(pq[0:64, 64 * h:64 * h + 64], po["Kh"][:, hc(h)], po["Vb"][:, hc(h)], start=False, stop=True)
        P.copy('act', PQG[g][:, :, 64:128], pq[0:64, 0:256].rearrange("p (h i) -> p h i", i=64))
        yield
        if readout:
            pgt = bk()
            for h in range(4):
                P.mm(pgt[0:64, 128 * h:128 * h + 128], Z[:, h, 0:64], M["Mrb"][:, h, :], start=True, stop=False)
                P.mm(pgt[0:64, 128 * h:128 * h + 128], po["Rt"][:, hc(h)], identb[:], start=False, stop=True)
            P.copy('act', PQG[g][:, :, 128:256], pgt[0:64, :].rearrange("p (h t) -> p h t", t=128))
            yield

    def yloc_mm(py, po, zs, g, with_state, sidx):
        Z = zs[g]
        M = Mn[g]
        for h in range(4):
            c = slice(256 * g + 64 * h, 256 * g + 64 * h + 64)
            o = py[:, 64 * h:64 * h + 64]
            P.mm(o, M["Mrb"][:, h, :], Z[:, h, 64:128], start=True, stop=False)
            P.mm(o, M["Mrk"][:, h, :], po["Vb"][:, c], start=False, stop=not with_state)
            if with_state:
                P.mm(o, PQG[g][:, h, 128:256], S[g][:, h, sidx, :], start=False, stop=True)

    def front1_flag(si):
        yield from make_h(si, si < 4)
        yield from big_mm(320, pv[si % 2])

    def front2_flag(si):
        par = si % 2
        pvx = pv[par]
        po, zs, gm = PO[par], Zall[par], gam[par]
        yield from prep_common(po, pvx)
        f1 = fls[:, si, 0:1]
        P.stt(Msel[par][:], D1[:], f1, SU[:], ALU.mult, ALU.add)
        P.tt('pool', MselT[par][:], SU[:], SL[:], ALU.add)
        P.tt('pool', MselT[par][:], MselT[par][:], Msel[par][:], ALU.subtract)
        P.stt(Trisel[:], D2[:], f1, Tri[:], ALU.mult, ALU.add)
        P.ts('pool', fbm[par][:, :, 0, :], ones64[:], fls[0:64, si, 1:2], ALU.mult)
        P.ts('pool', fbm[par][:, :, 1, :], ones64[:], fls[0:64, si, 0:1], ALU.mult)
        pw, pa = lora('flag', si, pvx)
        yield
        yield from prep_dir(po, zs, gm, pw, pa, Trisel[:], False, False, pvx)

    def chunk_flag(si, g):
        par = si % 2
        po, zs, gm = PO[par], Zall[par], gam[par]
        yield from chunk_group(po, zs, gm, g, Msel[par][:], MselT[par][:], None, False)
        ps_ = bank_g[g]()
        for h in range(4):
            P.mm(ps_[0:64, 128 * h:128 * h + 128], PQG[g][:, h, 0:64],
                 S[g][:, h, :, :].rearrange("p a i -> p (a i)"))
        psv = ps_[0:64, :].rearrange("p (h a i) -> p h a i", a=2, i=64)
        P.tt('dve', rt1[g][:], psv, PQG[g][:, :, 64:128].unsqueeze(2).to_broadcast([64, 4, 2, 64]), ALU.add)
        P.tt('pool', rt1[g][:], rt1[g][:], S[g][:], ALU.subtract)
        P.tt('pool', rt1[g][:], rt1[g][:], fbm[par][:], ALU.mult)
        P.tt('pool', S[g][:], S[g][:], rt1[g][:], ALU.add)
        yield

    nfl = LIM['nflag']
    for si in range(nfl + 2):
        tasks = []
        if 0 <= si - 2 < nfl:
            tasks += [chunk_flag(si - 2, 0), chunk_flag(si - 2, 1)]
        if 0 <= si - 1 < nfl:
            tasks.append((front2_flag(si - 1), PACE['f2']))
        if si < nfl:
            tasks.append((front1_flag(si), PACE['f1']))
        run_tasks(tasks)
    if hp == 0:
        dbg("S0", S[0][:], [64, 4, 2, 64])
        dbg("S1", S[1][:], [64, 4, 2, 64])

    def own_A1(i):
        si = NFLAG + i
        yp = ypart[i % 2]
        pvx = pv[i % 2]
        yield from make_h(si, False)
        yield from big_mm(NCOLT, pvx)
        if hp == 0 and i == 0:
            dbg("pn", pvx.pn[:], [128, 4 * NCOLT])
        pga = bank_f1()
        for k in range(8):
            P.mm(pga[:, 0:CH], hT[:, k, 64:192], Wg[:, k, :], start=(k == 0), stop=(k == 7))
        sigmoid_from(yp[:, 2, :], pga[:, 0:CH])
        P.tt('dve', yp[:, 2, :], yp[:, 2, :], pga[:, 0:CH], ALU.mult)
        yield

    def own_A2(i):
        yp = ypart[i % 2]
        pvx = pv[i % 2]
        yield from prep_common(PO[0], pvx)
        pw, pa = lora('own', 0, pvx)
        yield
        yield from prep_dir(PO[0], Zall[0], gam[0], pw, pa, Tri[:], True, False, pvx)
        P.tt('pool', T[1][:], pvx.r, T[4][:], ALU.mult)
        P.tt('pool', T[1][:], T[1][:], rkrow[:], ALU.mult)
        P.reduce(small[:, 2, :], h3(T[1][:]))
        P.tt('pool', h3(yp[:, 1, :]), h3(pvx.v), small[:, 2, :].unsqueeze(2).to_broadcast([128, HP, 64]), ALU.mult)
        yield

    def own_B(i):
        pvx = pv[i % 2]
        P.copy('pool', PO[1]["Vb"][:], PO[0]["Vb"][:])
        pw, pa = lora('own', 1, pvx)
        yield
        yield from prep_dir(PO[1], Zall[1], gam[1], pw, pa, TriT[:], True, True, pvx)

    def own_C(i, d, g):
        yp = ypart[i % 2]
        po, zs, gm = PO[d], Zall[d], gam[d]
        if d == 0:
            yield from chunk_group(po, zs, gm, g, SU[:], SL[:], Tri[:], True)
            py = bank_g[g]()
            yloc_mm(py, po, zs, g, True, 0)
            P.copy('act', yp[:, 0, 256 * g:256 * g + 256], py[:, 0:256])
            yield
            ps_ = bank_g[g]()
            for h in range(4):
                P.mm(ps_[0:64, 64 * h:64 * h + 64], PQG[g][:, h, 0:64], S[g][:, h, 0, :])
            P.tt('dve', S[g][:, :, 0, :], ps_[0:64, 0:256].rearrange("p (h i) -> p h i", i=64),
                 PQG[g][:, :, 64:128], ALU.add)
            yield
        else:
            yield from chunk_group(po, zs, gm, g, SL[:], SU[:], SL[:], True)
            py = bank_g[g]()
            yloc_mm(py, po, zs, g, False, 1)
            P.tt('dve', yp[:, 0, 256 * g:256 * g + 256], yp[:, 0, 256 * g:256 * g + 256], py[:, 0:256], ALU.add)
            P.dma(scr_pqg[i, g], PQG[g][:], writes=["scr_pqg%d_%d" % (i, g)])
            yield

    nown = LIM['nown']
    if nown > 0:
        run_tasks([own_A1(0)])
        run_tasks([own_A2(0)])
    for i in range(nown):
        run_tasks([own_C(i, 0, 0), own_C(i, 0, 1), (own_B(i), PACE['f2']),
                   (own_A1(i + 1) if i + 1 < nown else None, PACE['f1'])])
        run_tasks([own_C(i, 1, 0), own_C(i, 1, 1), (own_A2(i + 1) if i + 1 < nown else None, PACE['f2'])])
        P.dma(scr_y[i], ypart[i % 2][:], writes=["scr_y%d" % i])

    lnxw, lnxb = T[2], T[3]
    if LIM['passB'] and nown > 0:
        P.dma(lnxw[:], G['lnxw_d'][hp])
        P.dma(lnxb[:], G['lnxb_d'][hp])
    for i in (range(nown - 1, -1, -1) if LIM['passB'] else []):
        yld = ypart[i % 2]
        P.dma(yld[:], scr_y[i], reads=["scr_y%d" % i])
        for g in range(2):
            P.dma(PQG[g][:], scr_pqg[i, g], reads=["scr_pqg%d_%d" % (i, g)])
            py = bank_g[g]()
            for h in range(4):
                P.mm(py[:, 64 * h:64 * h + 64], PQG[g][:, h, 128:256], S[g][:, h, 1, :])
            P.tt('dve', yld[:, 0, 256 * g:256 * g + 256], yld[:, 0, 256 * g:256 * g + 256], py[:, 0:256], ALU.add)
            ps_ = bank_g[g]()
            for h in range(4):
                P.mm(ps_[0:64, 64 * h:64 * h + 64], PQG[g][:, h, 0:64], S[g][:, h, 1, :])
            P.tt('dve', S[g][:, :, 1, :], ps_[0:64, 0:256].rearrange("p (h i) -> p h i", i=64),
                 PQG[g][:, :, 64:128], ALU.add)
        y3 = h3(yld[:, 0, :])
        if hp == 0 and i == 0:
            dbg("ysum", yld[:, 0, :], [128, CH])
        P.reduce(small[:, 0, :], y3)
        P.ts('dve', small[:, 0, :], small[:, 0, :], 1.0 / 64, ALU.mult)
        P.tt('dve', h3(T[0][:]), y3, small[:, 0, :].unsqueeze(2).to_broadcast([128, HP, 64]), ALU.subtract)
        P.tt('pool', T[1][:], T[0][:], T[0][:], ALU.mult)
        P.reduce(small[:, 1, :], h3(T[1][:]))
        P.ts('dve', small[:, 1, :], small[:, 1, :], 1.0 / 64, ALU.mult, 64e-5, ALU.add)
        P.act(small[:, 1, :], small[:, 1, :], AF.Sqrt)
        P.recip(small[:, 1, :], small[:, 1, :])
        P.tt('dve', h3(T[0][:]), h3(T[0][:]), small[:, 1, :].unsqueeze(2).to_broadcast([128, HP, 64]), ALU.mult)
        P.tt('pool', T[0][:], T[0][:], lnxw[:], ALU.mult)
        P.tt('pool', T[0][:], T[0][:], lnxb[:], ALU.add)
        P.tt('dve', T[0][:], T[0][:], yld[:, 1, :], ALU.add)
        P.tt('dve', yab[:], T[0][:], yld[:, 2, :], ALU.mult)
        if hp == 0 and i == 0:
            dbg("yab0", yab[:], [128, CH], BF16)
        psb = bank_g[0]()[:, :].bitcast(BF16)
        for cc in range(4):
            P.tr(psb[:, 128 * cc:128 * cc + 128], yab[:, 128 * cc:128 * cc + 128], identb[:])
        P.copy('act', yat[:], psb[:, 0:512].rearrange("p (c t) -> p c t", t=128))
        P.dma(scr_ya[:, 4 * hp:4 * hp + 4, 128 * i:128 * i + 128], yat[:], writes=["scr_ya"])


def phase2(nc, P, st, sbuf, bank, psb, G, dbg):
    gsh, gateB, onesb, ones32 = G['gsh'], G['gateB'], G['onesb'], G['ones32']
    x2T, edge, x_own, wblk, y_out, scr_ya = G['x2T'], G['edge'], G['x_own'], G['wblk'], G['y_out'], G['scr_ya']
    TB = 1024
    TH = TB + 32
    SEG = TH // 3
    EXP = AF.Exp

    cw = sbuf(st, "cw", [128, 8, 31])
    cv = sbuf(st, "cv", [128, 3, 8])
    fnws = sbuf(st, "fnws", [128, D])
    edg = sbuf(st, "edg", [128, 2])
    P.dma(cw[:], G['cwT'])
    P.dma(cv[:], G['cvec'])
    P.dma(fnws[:], G['fnw'])
    P.dma(edg[:], edge)

    x2k = [sbuf(st, "x2k%d" % i, [128, TH]) for i in range(2)]
    sq2 = sbuf(st, "sq2", [128, TH], BF16)
    rs2 = sbuf(st, "rs2", [128, TH])
    h2 = sbuf(st, "h2", [128, 8, TH], BF16)
    zt = sbuf(st, "zt", [128, 2, TH])
    sgt = sbuf(st, "sgt", [128, SEG])
    acc = sbuf(st, "acc", [128, 8, TB])
    ybT = sbuf(st, "ybT", [128, 8, TB], BF16)
    mgT = sbuf(st, "mgT", [128, 8, TB], BF16)
    yaT = sbuf(st, "yaT", [128, 8, TB], BF16)
    wst = sbuf(st, "wst", [128, 8, 512])
    wbf = [sbuf(st, "wbf%d" % i, [128, 8, 512], BF16) for i in range(3)]
    mean = sbuf(st, "mean", [128, 512])
    rstd = sbuf(st, "rstd", [128, 512])
    tmpA = sbuf(st, "tmpA", [128, 512])
    tmpB = sbuf(st, "tmpB", [128, 512])
    e1 = sbuf(st, "e1", [128, 512])
    e2 = sbuf(st, "e2", [128, 512])
    e3 = sbuf(st, "e3", [128, 512])
    xo = sbuf(st, "xo", [128, D])
    xnw = sbuf(st, "xnw", [128, D])
    junk = sbuf(st, "junk", [128, D])
    ssq = sbuf(st, "ssq", [128, 2])
    wcnt = [0]

    seq1 = [0, 1, 2, 3, 4, 5, 6, 10, 7, 11, 8, 12, 9, 13, 14, 15]
    seq = seq1 + seq1
    loaded = [0]
    usep = [0]

    def _issue(p):
        d_ = wbf[p % 3]
        P.dma(wst[:], wblk[seq[p]])
        P.copy('pool', d_[:], wst[:])

    def load_block(bi):
        p = usep[0]
        usep[0] += 1
        assert seq[p] == bi, (p, seq[p], bi)
        while loaded[0] <= min(p + 1, len(seq) - 1):
            _issue(loaded[0])
            loaded[0] += 1
        return wbf[p % 3]

    def sigmoid_from(out, in_):
        P.act(out, in_, EXP, scale=-1.0)
        P.act(out, out, AF.Ln, bias=ones32[:, 0:1])
        P.act(out, out, EXP, scale=-1.0)

    for hb in range(2):
        t0 = TB * hb
        P.dma(yaT[:], scr_ya[:, :, t0:t0 + TB], reads=["scr_ya"])
        pst = [bank(), bank(), bank()]
        for k in range(8):
            xk = x2k[k % 2]
            P.dma(xk[:], x2T[:, k, t0:t0 + TH])
            P.act(sq2[:], xk[:], AF.Square)
            for s in range(3):
                P.mm(pst[s][:, 0:SEG], onesb[:], sq2[:, SEG * s:SEG * s + SEG], start=(k == 0), stop=(k == 7))
        for s in range(3):
            P.ts('dve', rs2[:, SEG * s:SEG * s + SEG], pst[s][:, 0:SEG], 1.0 / D, ALU.mult, 1e-6, ALU.add)
        P.act(rs2[:], rs2[:], AF.Ln)
        P.act(rs2[:], rs2[:], EXP, scale=-0.5)
        for k in range(8):
            xk = x2k[k % 2]
            P.dma(xk[:], x2T[:, k, t0:t0 + TH])
            P.tt('dve', xk[:], xk[:], rs2[:], ALU.mult)
            P.ts('pool', h2[:, k, :], xk[:], gsh[:, k, 0:1], ALU.mult, gsh[:, k, 1:2], ALU.add)

        for j in range(4):
            wb = load_block(j)
            for cc in range(2):
                c = 2 * j + cc
                for s in range(3):
                    pu, pg = bank(), bank()
                    for k in range(8):
                        P.mm(pu[:, 0:SEG], wb[:, k, 128 * cc:128 * cc + 128], h2[:, k, SEG * s:SEG * s + SEG],
                             start=(k == 0), stop=(k == 7))
                    for k in range(8):
                        P.mm(pg[:, 0:SEG], wb[:, k, 256 + 128 * cc:256 + 128 * cc + 128],
                             h2[:, k, SEG * s:SEG * s + SEG], start=(k == 0), stop=(k == 7))
                    sigmoid_from(sgt[:], pg[:, 0:SEG])
                    P.tt('dve', zt[:, cc, SEG * s:SEG * s + SEG], pu[:, 0:SEG], sgt[:], ALU.mult)
                if hb == 0:
                    P.ts('pool', zt[:, cc, 0:16], zt[:, cc, 0:16], edg[:, 0:1], ALU.mult)
                else:
                    P.ts('pool', zt[:, cc, TH - 16:TH], zt[:, cc, TH - 16:TH], edg[:, 1:2], ALU.mult)
                P.ts('dve', acc[:, c, :], zt[:, cc, 1:1 + TB], cw[:, c, 0:1], ALU.mult, cv[:, 0, c:c + 1], ALU.add)
                for k in range(1, 31):
                    P.stt(acc[:, c, :], zt[:, cc, k + 1:k + 1 + TB], cw[:, c, k:k + 1], acc[:, c, :],
                          ALU.mult, ALU.add)
        if hb == 0:
            dbg("acc", acc[:], [128, 8, TB])
        wgb = [load_block(4), load_block(5)]
        for s in range(2):
            sl = slice(512 * s, 512 * s + 512)
            p1, p2 = bank(), bank()
            for c in range(8):
                P.mm(p1[:, :], ones32[:], acc[:, c, sl], start=(c == 0), stop=(c == 7))
            for c in range(8):
                P.act(e2[:], acc[:, c, sl], AF.Square)
                P.mm(p2[:, :], ones32[:], e2[:], start=(c == 0), stop=(c == 7))
            P.ts('dve', mean[:], p1[:, :], 1.0 / D, ALU.mult)
            P.tt('dve', e3[:], mean[:], mean[:], ALU.mult)
            P.stt(rstd[:], p2[:, :], 1.0 / D, e3[:], ALU.mult, ALU.subtract)
            P.ts('dve', rstd[:], rstd[:], 1e-5, ALU.add)
            P.act(rstd[:], rstd[:], AF.Sqrt)
            P.recip(rstd[:], rstd[:])
            for c in range(8):
                pg = bank()
                for k in range(8):
                    P.mm(pg[:, :], wgb[c // 4][:, k, 128 * (c % 4):128 * (c % 4) + 128],
                         h2[:, k, 16 + 512 * s:16 + 512 * s + 512], start=(k == 0), stop=(k == 7))
                sigmoid_from(e1[:], pg[:, :])
                P.tt('dve', e1[:], pg[:, :], e1[:], ALU.mult)
                P.tt('dve', tmpA[:], acc[:, c, sl], mean[:], ALU.subtract)
                P.tt('pool', tmpA[:], tmpA[:], rstd[:], ALU.mult)
                P.ts('pool', tmpA[:], tmpA[:], cv[:, 1, c:c + 1], ALU.mult, cv[:, 2, c:c + 1], ALU.add)
                sigmoid_from(tmpB[:], tmpA[:])
                P.tt('pool', tmpA[:], tmpA[:], tmpB[:], ALU.mult)
                P.tt('pool', ybT[:, c, sl], tmpA[:], e1[:], ALU.mult)
        if hb == 0:
            dbg("ybT", ybT[:], [128, 8, TB], BF16)
        for rnd in range(2):
            for jb in range(2):
                wm_b = load_block((6 if rnd == 0 else 8) + jb)
                wp_b = load_block((10 if rnd == 0 else 12) + jb)
                for mm_ in range(4):
                    m = 4 * jb + mm_
                    for s in range(2):
                        sl = slice(512 * s, 512 * s + 512)
                        pm_, pp_ = bank(), bank()
                        for k in range(8):
                            P.mm(pm_[:, :], wm_b[:, k, 128 * mm_:128 * mm_ + 128],
                                 h2[:, k, 16 + 512 * s:16 + 512 * s + 512], start=(k == 0), stop=(k == 7))
                        for k in range(8):
                            rhs = (yaT if rnd == 0 else ybT)[:, k, sl]
                            P.mm(pp_[:, :], wp_b[:, k, 128 * mm_:128 * mm_ + 128], rhs, start=(k == 0), stop=(k == 7))
                        sigmoid_from(e1[:], pm_[:, :])
                        if rnd == 0:
                            P.tt('dve', mgT[:, m, sl], pp_[:, :], e1[:], ALU.mult)
                        else:
                            P.tt('dve', e2[:], pp_[:, :], e1[:], ALU.mult)
                            P.tt('pool', mgT[:, m, sl], mgT[:, m, sl], e2[:], ALU.add)
        wo = [load_block(14), load_block(15)]
        for tt_ in range(8):
            ti = 8 * hb + tt_
            P.dma(xo[:], x_own[ti])
            for n in range(2):
                po_ = bank()
                for k in range(8):
                    P.mm(po_[:, :], mgT[:, k, 128 * tt_:128 * tt_ + 128], wo[n][:, k, :], start=(k == 0), stop=(k == 7))
                sl = slice(512 * n, 512 * n + 512)
                P.tt('dve', xnw[:, sl], po_[:, :], gateB[:, sl], ALU.mult)
                P.tt('pool', xnw[:, sl], xnw[:, sl], xo[:, sl], ALU.add)
            P.act(junk[:], xnw[:], AF.Square, accum_out=ssq[:, 0:1])
            P.ts('dve', ssq[:, 1:2], ssq[:, 0:1], 1.0 / D, ALU.mult, 1e-6, ALU.add)
            P.act(ssq[:, 1:2], ssq[:, 1:2], AF.Sqrt)
            P.recip(ssq[:, 1:2], ssq[:, 1:2])
            P.stt(junk[:], xnw[:], ssq[:, 1:2], fnws[:], ALU.mult, ALU.mult)
            P.dma(y_out[ti], junk[:])


def _bc(v, n=128):
    v = np.asarray(v, np.float32)
    return np.ascontiguousarray(np.broadcast_to(v[None], (n,) + v.shape))


def _fm(a):
    t = a.shape[0]
    return np.ascontiguousarray(a.T.reshape(8, 128, t).transpose(1, 0, 2))


def prep_inputs(inp):
    f32 = np.float32
    x, c, ctx, c_ctx = [np.asarray(inp[k], f32) for k in ("x", "c", "ctx", "c_ctx")]
    w_in = np.asarray(inp["w_in"], f32)[0]
    mu_shift = np.asarray(inp["mu_shift"], f32)[0]
    w0, w2, a0, a2 = [np.asarray(inp[k], f32)[0] for k in ("w0", "w2", "a0", "a2")]
    k_k, k_a = np.asarray(inp["k_k"], f32)[0], np.asarray(inp["k_a"], f32)[0]
    r_k = np.asarray(inp["r_k"], f32)[0].reshape(-1)
    lnx_w, lnx_b = np.asarray(inp["lnx_w"], f32)[0], np.asarray(inp["lnx_b"], f32)[0]
    conv_w, conv_b = np.asarray(inp["conv_w"], f32)[0], np.asarray(inp["conv_b"], f32)[0]
    cln_w, cln_b = np.asarray(inp["cln_w"], f32)[0], np.asarray(inp["cln_b"], f32)[0]
    w_mod, b_mod = np.asarray(inp["w_mod"], f32)[0], np.asarray(inp["b_mod"], f32)[0]
    norm_w = np.asarray(inp["norm_w"], f32)[0]
    wpa, wpb, wout = [np.asarray(inp[k], f32)[0] for k in ("w_proj_a", "w_proj_b", "w_out")]
    fnw = np.asarray(inp["final_norm_w"], f32)

    def wk(cols):
        return np.ascontiguousarray(w_in[:, cols].reshape(8, 128, -1).transpose(1, 0, 2))

    shared = {}
    shared["w_mod"] = np.ascontiguousarray(w_mod.reshape(8, 128, 3072).transpose(1, 0, 2))
    shared["b_modT"] = np.ascontiguousarray(b_mod.reshape(24, 128).T)
    shared["bgate"] = _bc(b_mod[2048:3072])
    shared["norm_wT"] = np.ascontiguousarray(norm_w.reshape(8, 128).T)
    wrk = np.zeros((NPASS, 128, 8, 4, NCOLT), f32)
    murk = np.zeros((NPASS, 128, 4, NCOLT), f32)
    wga = np.zeros((NPASS, 128, 8, CH), f32)
    rows = {n: np.zeros((NPASS, 128, CH), f32) for n in ("kkrow", "karow", "rkrow", "lnxw", "lnxb", "w2s", "a2s")}
    w0s = np.zeros((NPASS, 33, CH), f32)
    a0s = np.zeros((NPASS, 33, CH), f32)
    for hp in range(NPASS):
        c0 = CH * hp
        for X in range(4):
            m128 = 4 * np.arange(128) + X
            m64 = 4 * np.arange(64) + X
            cols = np.concatenate([1024 + c0 + m128, 2048 + c0 + m128, 3072 + m64, c0 + m128])
            wrk[hp, :, :, X, :] = wk(cols)
            murk[hp, :, X, :] = mu_shift[cols][None, :]
        wga[hp] = wk(3328 + c0 + np.arange(CH))
        sl = slice(c0, c0 + CH)
        rows["kkrow"][hp] = k_k[sl][None]
        rows["karow"][hp] = k_a[sl][None]
        rows["rkrow"][hp] = r_k[sl][None]
        rows["lnxw"][hp] = lnx_w[sl][None]
        rows["lnxb"][hp] = lnx_b[sl][None]
        rows["w2s"][hp] = w2[:, :, sl].reshape(128, CH)
        rows["a2s"][hp] = a2[:, :, sl].reshape(128, CH)
        w0s[hp, 0], w0s[hp, 32] = w0[0, sl], w0[1, sl]
        a0s[hp, 0], a0s[hp, 32] = a0[0, sl], a0[1, sl]
    shared.update(wrk=wrk, mu_rk=murk, wga=wga, w0s=w0s, a0s=a0s, **rows)
    blocks = []
    for j in range(4):
        blocks.append(wk(np.concatenate([4352 + 256 * j + np.arange(256), 5376 + 256 * j + np.arange(256)])))
    for base in (6400, 7424, 8448):
        for j in range(2):
            blocks.append(wk(base + 512 * j + np.arange(512)))
    for wmat in (wpa, wpb, wout):
        for j in range(2):
            blocks.append(np.ascontiguousarray(wmat[:, 512 * j:512 * j + 512].reshape(8, 128, 512).transpose(1, 0, 2)))
    shared["wblk"] = np.stack(blocks)
    shared["cwT"] = np.ascontiguousarray(conv_w.T.reshape(8, 128, 31).transpose(1, 0, 2))
    shared["cvec"] = np.ascontiguousarray(np.stack([v.reshape(8, 128).T for v in (conv_b, cln_w, cln_b)], axis=1))
    shared["fnw"] = _bc(fnw)

    def slot(seq, n):
        L = seq.shape[0]
        lo, hi = 128 * n - 64, 128 * n + 192
        buf = np.zeros((256, D), f32)
        val = np.zeros((256,), f32)
        a, b_ = max(lo, 0), min(hi, L)
        buf[a - lo:b_ - lo] = seq[a:b_]
        val[a - lo:b_ - lo] = 1.0
        return _fm(buf), val

    in_maps = []
    for core in range(8):
        b, q = core // 4, core % 4
        lat, cx = x[b], ctx[b]
        sl_list = [(cx, 0, 0), (cx, 1, 0), (cx, 1, 1), (cx, 0, 1)]
        sl_list += [(lat, n, 0) for n in range(16 * q)]
        sl_list += [(lat, n, 1) for n in range(63, 16 * q + 15, -1)]
        assert len(sl_list) == NFLAG
        sl_list += [(lat, 16 * q + i, None) for i in range(NOWN)]
        xs_ = np.zeros((NSLOT, 128, 8, 256), f32)
        va_ = np.zeros((NSLOT, 128, 256), f32)
        fl_ = np.zeros((128, NFLAG, 3), f32)
        for s, (seq, n, f) in enumerate(sl_list):
            xs_[s], v = slot(seq, n)
            va_[s] = v[None, :]
            if f is not None:
                fl_[:, s, 0] = f
                fl_[:, s, 1] = 1 - f
                fl_[0, s, 2] = 1 - f
                fl_[32, s, 2] = f
        lo = 2048 * q - 16
        buf = np.zeros((2080, D), f32)
        a, b_ = max(lo, 0), min(lo + 2080, SEQ)
        buf[a - lo:b_ - lo] = lat[a:b_]
        m = dict(shared)
        m.update(xs=xs_, valid=va_, fl=fl_, x2T=_fm(buf),
                 edge=_bc(np.array([1.0 if q > 0 else 0.0, 1.0 if q < 3 else 0.0], f32)),
                 x_own=np.ascontiguousarray(lat[2048 * q:2048 * q + 2048].reshape(NOWN, 128, D)),
                 c2=np.ascontiguousarray(np.stack([c[b].reshape(8, 128).T, c_ctx.reshape(8, 128).T], axis=2)))
        in_maps.append(m)
    return in_maps


_NC_CACHE = {}


def kernel(**inputs):
    if "nc" not in _NC_CACHE:
        _NC_CACHE["nc"] = build()[0]
    nc = _NC_CACHE["nc"]
    in_maps = prep_inputs(inputs)
    res = run_bass_kernel_spmd(nc, in_maps, core_ids=list(range(8)))
    out = np.zeros((2, SEQ, D), np.float32)
    for core in range(8):
        b, q = core // 4, core % 4
        out[b, 2048 * q:2048 * q + 2048] = np.asarray(res.results[core]["y_out"]).reshape(2048, D)
    return out
```

```python
import numpy as np
from contextlib import ExitStack
import concourse.bass as bass
import concourse.mybir as mybir
from concourse.bass_utils import run_bass_kernel_spmd

F32 = mybir.dt.float32
BF16 = mybir.dt.bfloat16
AF = mybir.ActivationFunctionType
ALU = mybir.AluOpType
AX = mybir.AxisListType

SAME_ENG_SYNC = True
USE_F32R = False
N_DMA_SEMS = 16

D = 1024
NHEAD = 16
SEQ = 8192
CTXL = 256
HP = 8
NPASS = NHEAD // HP
CH = HP * 64
NFLAG = 52
NOWN = 16
NSLOT = NFLAG + NOWN
NCOLT = 448
EM05 = float(np.exp(-0.5))


class Prog:
    CENGS = ['pe', 'act', 'dve', 'pool']
    ENGS = ['pe', 'act', 'dve', 'pool', 'sp']

    def __init__(self, nc, stack):
        self.nc = nc
        self.streams = {e: [] for e in self.ENGS}
        self.cnt = {e: 0 for e in self.CENGS}
        self.sems = {}
        for e in self.CENGS:
            self.sems[e] = stack.enter_context(nc.semaphore("s_" + e))
        for i in range(N_DMA_SEMS):
            self.sems[('dma', i)] = stack.enter_context(nc.semaphore("s_dma%d" % i))
        self.dma_cnt = [0] * N_DMA_SEMS
        self.dma_rr = 0
        self.lastw = {}
        self.readers = {}
        self.nops = 0
        self.emitted = {e: 0 for e in self.ENGS}
        self.seen = {e: {} for e in self.ENGS}

    @staticmethod
    def _key(b):
        return b if isinstance(b, str) else b.tensor.name

    def op(self, eng, fn, reads=(), writes=(), dma=False):
        waits = {}

        def need(tok):
            k, v, e = tok
            if e == eng and (eng == 'pe' or not SAME_ENG_SYNC) and not dma:
                return
            if waits.get(k, 0) < v:
                waits[k] = v
        rk = [self._key(b) for b in reads]
        wk = [self._key(b) for b in writes]
        for b in rk:
            if b in self.lastw:
                need(self.lastw[b])
        for b in wk:
            if b in self.lastw:
                need(self.lastw[b])
            for t in self.readers.get(b, {}).values():
                need(t)
        if dma:
            i = self.dma_rr
            self.dma_rr = (i + 1) % N_DMA_SEMS
            prev = self.dma_cnt[i]
            if prev:
                waits[('dma', i)] = max(waits.get(('dma', i), 0), 16 * prev)
            self.dma_cnt[i] += 1
            tok = (('dma', i), 16 * self.dma_cnt[i], 'dma')
            inc = (('dma', i), 16)
        else:
            self.cnt[eng] += 1
            tok = (eng, self.cnt[eng], eng)
            inc = (eng, 1)
        self.streams[eng].append((waits, fn, inc))
        self.nops += 1
        for b in rk:
            d = self.readers.setdefault(b, {})
            old = d.get(tok[0])
            if old is None or old[1] < tok[1]:
                d[tok[0]] = tok
        for b in wk:
            self.lastw[b] = tok
            self.readers[b] = {}
        return tok

    def finish(self):
        waits = {}
        for i in range(N_DMA_SEMS):
            if self.dma_cnt[i]:
                waits[('dma', i)] = 16 * self.dma_cnt[i]
        for e in self.CENGS:
            if self.cnt[e]:
                waits[e] = self.cnt[e]
        for e in self.ENGS:
            self.streams[e].append((dict(waits), None, None))

    def emit(self):
        sems = self.sems
        streams = self.streams
        emitted = self.emitted
        seen_all = self.seen
        with self.nc.Block() as block:
            def run(name):
                def body(eng):
                    seen = seen_all[name]
                    lst = streams[name]
                    for waits, fn, inc in lst[emitted[name]:]:
                        for k, v in waits.items():
                            if seen.get(k, 0) >= v:
                                continue
                            seen[k] = v
                            eng.wait_ge(sems[k], v)
                        if fn is None:
                            continue
                        ins = fn(eng)
                        ins.then_inc(sems[inc[0]], inc[1])
                    emitted[name] = len(lst)
                return body
            block.tensor(run('pe'))
            block.scalar(run('act'))
            block.vector(run('dve'))
            block.gpsimd(run('pool'))
            block.sync(run('sp'))

    def dma(self, out, in_, reads=None, writes=None, eng='sp'):
        return self.op(eng, lambda e: e.dma_start(out=out, in_=in_),
                       reads if reads is not None else [in_],
                       writes if writes is not None else [out], dma=True)

    def mm(self, out, lhsT, rhs, start=True, stop=True, r32=False):
        if r32 and USE_F32R:
            l2, r2 = lhsT.bitcast(mybir.dt.float32r), rhs.bitcast(mybir.dt.float32r)
            return self.op('pe', lambda e: e.matmul(out, l2, r2, start=start, stop=stop), [lhsT, rhs], [out])
        return self.op('pe', lambda e: e.matmul(out, lhsT, rhs, start=start, stop=stop),
                       [lhsT, rhs], [out])

    def tr(self, out, in_, ident):
        return self.op('pe', lambda e: e.transpose(out, in_, ident), [in_, ident], [out])

    def act(self, out, in_, func, scale=1.0, bias=0.0, accum_out=None):
        rd = [in_]
        wr = [out]
        if not isinstance(scale, (int, float)):
            rd.append(scale)
        if not isinstance(bias, (int, float)):
            rd.append(bias)
        if accum_out is not None:
            wr.append(accum_out)
            f = lambda e: e.activation(out=out, in_=in_, func=func, scale=scale, bias=bias, accum_out=accum_out)
        else:
            f = lambda e: e.activation(out=out, in_=in_, func=func, scale=scale, bias=bias)
        return self.op('act', f, rd, wr)

    def tt(self, eng, out, in0, in1, op):
        return self.op(eng, lambda e: e.tensor_tensor(out=out, in0=in0, in1=in1, op=op), [in0, in1], [out])

    def ts(self, eng, out, in0, s1, op0, s2=None, op1=None):
        rd = [in0]
        if not isinstance(s1, (int, float)):
            rd.append(s1)
        if s2 is not None and not isinstance(s2, (int, float)):
            rd.append(s2)
        if op1 is None and eng == 'pool' and op0 == ALU.mult:
            f = lambda e: e.tensor_scalar(out=out, in0=in0, scalar1=s1, scalar2=0.0, op0=ALU.mult, op1=ALU.add)
        elif op1 is None and eng == 'pool' and op0 == ALU.add:
            f = lambda e: e.tensor_scalar(out=out, in0=in0, scalar1=s1, scalar2=1.0, op0=ALU.add, op1=ALU.mult)
        elif op1 is None:
            f = lambda e: e.tensor_scalar(out=out, in0=in0, scalar1=s1, scalar2=None, op0=op0)
        else:
            f = lambda e: e.tensor_scalar(out=out, in0=in0, scalar1=s1, scalar2=s2, op0=op0, op1=op1)
        return self.op(eng, f, rd, [out])

    def stt(self, out, in0, scalar, in1, op0, op1):
        rd = [in0, in1]
        if not isinstance(scalar, (int, float)):
            rd.append(scalar)
        return self.op('dve', lambda e: e.scalar_tensor_tensor(out=out, in0=in0, scalar=scalar, in1=in1, op0=op0, op1=op1),
                       rd, [out])

    def copy(self, eng, out, in_):
        if eng == 'act':
            return self.act(out, in_, AF.Copy)
        return self.op(eng, lambda e: e.tensor_copy(out=out, in_=in_), [in_], [out])

    def memset(self, eng, ap, val):
        return self.op(eng, lambda e: e.memset(ap, val), [], [ap])

    def recip(self, out, in_):
        return self.op('dve', lambda e: e.reciprocal(out=out, in_=in_), [in_], [out])

    def reduce(self, out, in_, op=ALU.add):
        return self.op('dve', lambda e: e.tensor_reduce(out=out, in_=in_, axis=AX.X, op=op), [in_], [out])

    def aselect(self, ap, cmp, fill, pattern, cm, base=0):
        return self.op('pool', lambda e: e.affine_select(out=ap, in_=ap, compare_op=cmp, fill=fill, base=base,
                                                         pattern=pattern, channel_multiplier=cm), [ap], [ap])


LIM = {'stop': None, 'nflag': NFLAG, 'nown': NOWN, 'passB': True, 'npass': NPASS, 'sub': 99}


def build(debug=()):
    nc = bass.Bass("TRN2", target_bir_lowering=False)
    dbg_specs = {}

    def din(name, shape, dt=F32):
        return nc.dram_tensor(name, list(shape), dt, kind="ExternalInput").ap()

    xs = din("xs", [NSLOT, 128, 8, 256])
    valid = din("valid", [NSLOT, 128, 256])
    fl = din("fl", [128, NFLAG, 3])
    x2T = din("x2T", [128, 8, 2080])
    edge = din("edge", [128, 2])
    x_own = din("x_own", [NOWN, 128, D])
    c2 = din("c2", [128, 8, 2])
    w_mod = din("w_mod", [128, 8, 3072])
    b_modT = din("b_modT", [128, 24])
    bgate = din("bgate", [128, D])
    norm_wT = din("norm_wT", [128, 8])
    wrk = din("wrk", [NPASS, 128, 8, 4, NCOLT])
    mu_rk = din("mu_rk", [NPASS, 128, 4, NCOLT])
    wga = din("wga", [NPASS, 128, 8, CH])
    kkrow_d = din("kkrow", [NPASS, 128, CH])
    karow_d = din("karow", [NPASS, 128, CH])
    rkrow_d = din("rkrow", [NPASS, 128, CH])
    lnxw_d = din("lnxw", [NPASS, 128, CH])
    lnxb_d = din("lnxb", [NPASS, 128, CH])
    w2s_d = din("w2s", [NPASS, 128, CH])
    a2s_d = din("a2s", [NPASS, 128, CH])
    w0s_d = din("w0s", [NPASS, 33, CH])
    a0s_d = din("a0s", [NPASS, 33, CH])
    wblk = din("wblk", [16, 128, 8, 512])
    cwT = din("cwT", [128, 8, 31])
    cvec = din("cvec", [128, 3, 8])
    fnw = din("fnw", [128, D])
    y_out = nc.dram_tensor("y_out", [NOWN, 128, D], F32, kind="ExternalOutput").ap()
    scr_pqg = nc.dram_tensor("scr_pqg", [NOWN, 2, 64, 4, 256], F32, kind="Internal").ap()
    scr_y = nc.dram_tensor("scr_y", [NOWN, 128, 3, CH], F32, kind="Internal").ap()
    scr_ya = nc.dram_tensor("scr_ya", [128, 8, 2048], BF16, kind="Internal").ap()

    with ExitStack() as top:
        P = Prog(nc, top)

        uid = [0]

        def sbuf(st, name, shape, dt=F32):
            uid[0] += 1
            return st.enter_context(nc.sbuf_tensor("%s_u%d" % (name, uid[0]), list(shape), dt))

        dbg_out = {}

        def dbg(name, ap, shape, dt=F32):
            if name not in debug:
                return
            o = nc.dram_tensor("dbg_" + name, list(shape), dt, kind="ExternalOutput").ap()
            dbg_out[name] = o
            P.dma(o, ap)

        banks = [top.enter_context(nc.psum_tensor("ps%d" % i, [128, 512], F32)) for i in range(8)]
        psb = None
        bank_rr = [0]

        def bank():
            b = banks[bank_rr[0] % 8]
            bank_rr[0] += 1
            return b

        ident = sbuf(top, "ident", [128, 128])
        identb = sbuf(top, "identb", [128, 128], BF16)
        ones32 = sbuf(top, "ones32", [128, 128])
        onesb = sbuf(top, "onesb", [128, 128], BF16)
        gsh = sbuf(top, "gsh", [128, 8, 4])
        gateB = sbuf(top, "gateB", [128, D])

        P.memset('pool', ident[:], 0.0)
        P.aselect(ident[:], ALU.not_equal, 1.0, [[-1, 128]], 1)
        P.copy('pool', identb[:], ident[:])
        P.memset('pool', ones32[:], 1.0)
        P.memset('pool', onesb[:], 1.0)

        with ExitStack() as st:
            wm = sbuf(st, "wm", [128, 8, 3072])
            c2s = sbuf(st, "c2s", [128, 8, 2])
            sc = sbuf(st, "sc", [128, 8, 2])
            scB = sbuf(st, "scB", [128, 8, 128])
            bm = sbuf(st, "bm", [128, 24])
            nw = sbuf(st, "nw", [128, 8])
            modT = sbuf(st, "modT", [128, 24, 2])
            bg = sbuf(st, "bg", [128, D])
            for k in range(8):
                P.dma(wm[:, k, :], w_mod[:, k, :])
            P.dma(c2s[:], c2)
            P.dma(bm[:], b_modT)
            P.dma(nw[:], norm_wT)
            P.dma(bg[:], bgate)
            P.act(sc[:], c2s[:], AF.Exp, scale=-1.0)
            P.ts('dve', sc[:], sc[:], 1.0, ALU.add)
            P.recip(sc[:], sc[:])
            P.tt('dve', sc[:], sc[:], c2s[:], ALU.mult)
            pm = bank()
            for j in (range(24) if LIM['sub'] >= 2 else []):
                for k in range(8):
                    P.mm(pm[:, 2 * j:2 * j + 2], wm[:, k, 128 * j:128 * j + 128], sc[:, k, :],
                         start=(k == 0), stop=(k == 7))
            if LIM['sub'] >= 3:
                P.tt('dve', modT[:], pm[:, 0:48].rearrange("p (j t) -> p j t", t=2),
                     bm[:].unsqueeze(2).to_broadcast([128, 24, 2]), ALU.add)
            for t_, (go, so) in (enumerate([(0, 1), (2, 3)]) if LIM['sub'] >= 4 else []):
                P.ts('dve', gsh[:, :, go], modT[:, 8:16, t_], 1.0, ALU.add)
                P.tt('dve', gsh[:, :, go], gsh[:, :, go], nw[:], ALU.mult)
                P.copy('dve', gsh[:, :, so], modT[:, 0:8, t_])
            if LIM['sub'] >= 5:
                P.copy('dve', scB[:], sc[:, :, 0:1].to_broadcast([128, 8, 128]))
            for n in (range(2) if LIM['sub'] >= 6 else []):
                pg = bank()
                for k in range(8):
                    P.mm(pg[:, :], scB[:, k, :], wm[:, k, 2048 + 512 * n:2048 + 512 * n + 512],
                         start=(k == 0), stop=(k == 7))
                P.tt('dve', gateB[:, 512 * n:512 * n + 512], pg[:, :], bg[:, 512 * n:512 * n + 512], ALU.add)
            dbg("sc", sc[:], [128, 8, 2])
            dbg("gsh", gsh[:], [128, 8, 4])
            dbg("gateB", gateB[:], [128, D])
            if LIM['stop'] == 'p0':
                P.finish()
            P.emit()
        if LIM['stop'] == 'p0':
            return nc, list(dbg_out.keys())

        for hp in range(LIM['npass']):
            with ExitStack() as st:
                phase1(nc, P, st, sbuf, bank, psb, hp, locals(), dbg)
                if LIM['stop'] == 'p1' and hp == LIM['npass'] - 1:
                    P.finish()
                P.emit()
        if LIM['stop'] == 'p1':
            return nc, list(dbg_out.keys())

        with ExitStack() as st:
            phase2(nc, P, st, sbuf, bank, psb, locals(), dbg)
            P.finish()
            P.emit()
    return nc, list(dbg_out.keys())


PACE = {'f1': 2, 'f2': 2}


def run_tasks(tasks):
    active = []
    for t in tasks:
        if t is None:
            continue
        if isinstance(t, tuple):
            if t[0] is not None:
                active.append([t[0], t[1]])
        else:
            active.append([t, 1])
    rnd = 0
    while active:
        only_slow = all(st_ > 1 for _, st_ in active)
        for item in list(active):
            g_, st_ = item
            if st_ > 1 and not only_slow and rnd % st_ != 0:
                continue
            try:
                next(g_)
            except StopIteration:
                active.remove(item)
        rnd += 1


def phase1(nc, P, st, sbuf, bank, psb, hp, G, dbg):
    xs, valid, fl = G['xs'], G['valid'], G['fl']
    ident, identb, ones32, onesb, gsh = G['ident'], G['identb'], G['ones32'], G['onesb'], G['gsh']
    scr_pqg, scr_y, scr_ya = G['scr_pqg'], G['scr_y'], G['scr_ya']
    banks = G['banks']

    def mk_bank(lst):
        idx = [0]

        def f():
            b_ = lst[idx[0] % len(lst)]
            idx[0] += 1
            return b_
        return f
    bank_f1 = mk_bank(banks[0:2])
    bank_f = mk_bank(banks[2:4])
    bank_g = [mk_bank(banks[4:6]), mk_bank(banks[6:8])]

    W = sbuf(st, "W", [128, 8, 4, NCOLT], BF16)
    Wg = sbuf(st, "Wg", [128, 8, CH], BF16)
    mu = sbuf(st, "mu", [128, 4, NCOLT])
    kkrow = sbuf(st, "kkrow", [128, CH])
    karow = sbuf(st, "karow", [128, CH])
    rkrow = sbuf(st, "rkrow", [128, CH])
    w2s = sbuf(st, "w2s", [128, CH])
    a2s = sbuf(st, "a2s", [128, CH])
    w0s = sbuf(st, "w0s", [33, CH])
    a0s = sbuf(st, "a0s", [33, CH])
    ones33 = sbuf(st, "ones33", [33, 128])
    fls = sbuf(st, "fls", [128, NFLAG, 3])
    xb = sbuf(st, "xb", [128, 8, 256])
    xbf = xb[:].rearrange("p k t -> p (k t)")
    Wst = xbf[:, 0:4 * NCOLT].rearrange("p (x n) -> p x n", x=4)
    for k in range(8):
        P.dma(Wst, G['wrk'][hp, :, k, :, :])
        P.copy('pool', W[:, k, :, :], Wst)
    for k in range(8):
        P.dma(xbf[:, 0:CH], G['wga'][hp, :, k, :])
        P.copy('pool', Wg[:, k, :], xbf[:, 0:CH])
    for dst, src in [(mu, G['mu_rk']), (kkrow, G['kkrow_d']), (karow, G['karow_d']), (rkrow, G['rkrow_d']),
                     (w2s, G['w2s_d']), (a2s, G['a2s_d']), (w0s, G['w0s_d']), (a0s, G['a0s_d'])]:
        P.dma(dst[:], src[hp])
    P.dma(fls[:], fl)
    P.memset('pool', ones33[:], 1.0)
    sel33 = [sbuf(st, "sel33_%d" % d, [33, 128]) for d in range(2)]
    for d in range(2):
        P.memset('pool', sel33[d][:], 0.0)
        P.memset('pool', sel33[d][32 * d:32 * d + 1, :], 1.0)

    Tri = sbuf(st, "Tri", [128, 128])
    TriT = sbuf(st, "TriT", [128, 128])
    SU = sbuf(st, "SU", [128, 128])
    SL = sbuf(st, "SL", [128, 128])
    D1 = sbuf(st, "D1", [128, 128])
    D2 = sbuf(st, "D2", [128, 128])
    mL = sbuf(st, "mL", [128, 128], BF16)
    mR = sbuf(st, "mR", [128, 128], BF16)
    ones64 = sbuf(st, "ones64", [64, 4, 64])
    for t_, cmp, pat, cm in [(Tri, ALU.is_ge, [[1, 128]], -1), (TriT, ALU.is_ge, [[-1, 128]], 1),
                             (SU, ALU.is_gt, [[1, 128]], -1), (SL, ALU.is_gt, [[-1, 128]], 1)]:
        P.memset('pool', t_[:], 1.0)
        P.aselect(t_[:], cmp, 0.0, pat, cm)
    P.tt('pool', D1[:], SL[:], SU[:], ALU.subtract)
    P.tt('pool', D2[:], TriT[:], Tri[:], ALU.subtract)
    P.memset('pool', mL[:], 1.0)
    P.memset('pool', mR[:], 1.0)
    for c0 in (0, 64):
        P.memset('pool', mL[:, c0:c0 + 1], 0.0)
        P.memset('pool', mR[:, c0 + 63:c0 + 64], 0.0)
    P.memset('pool', ones64[:], 1.0)

    vb = sbuf(st, "vb", [128, 256])
    sq = sbuf(st, "sq", [128, 8, 256], BF16)
    rs = sbuf(st, "rs", [128, 256])
    hT = sbuf(st, "hT", [128, 8, 256], BF16)
    hsh1 = sbuf(st, "hsh", [128, 8, 128], BF16)
    hsh = [hsh1, hsh1]
    hd = [sbuf(st, "hd%d" % i, [128, 8, 128], BF16) for i in range(4)]
    pnb = [sbuf(st, "pn%d" % i, [128, 4 * NCOLT]) for i in range(2)]
    mtmp = sbuf(st, "mtmp", [128, NCOLT])
    T = [sbuf(st, "T%d" % i, [128, CH]) for i in range(9)]
    small = sbuf(st, "small", [128, 4, 8])
    lsel = sbuf(st, "lsel", [128, 2, 64])
    ltmp = sbuf(st, "ltmp", [128, 2, 64])
    LW = sbuf(st, "LW", [128, 128])
    LA = sbuf(st, "LA", [128, 128])
    LWT = sbuf(st, "LWT", [128, 128])
    LAT = sbuf(st, "LAT", [128, 128])
    frow = sbuf(st, "frow", [33, 128])
    Trisel = sbuf(st, "Trisel", [128, 128])
    one11 = sbuf(st, "one11", [1, 1])
    P.memset('pool', one11[:], 1.0)
    names = ["Bt", "Kt", "Rt", "Bh", "Kh", "Vb"]
    PO = [{n: sbuf(st, "%s%d" % (n, i), [128, CH], BF16) for n in names} for i in range(2)]
    Zall = [[sbuf(st, "Z%d_%d" % (i, g), [128, 4, 128], BF16) for g in range(2)] for i in range(2)]
    gam = [sbuf(st, "gam%d" % i, [64, HP]) for i in range(2)]
    Msel = [sbuf(st, "Msel%d" % i, [128, 128]) for i in range(2)]
    MselT = [sbuf(st, "MselT%d" % i, [128, 128]) for i in range(2)]
    fbm = [sbuf(st, "fbm%d" % i, [64, 4, 2, 64]) for i in range(2)]
    ypart = [sbuf(st, "ypart%d" % i, [128, 3, CH]) for i in range(2)]
    featT = [sbuf(st, "featT%d" % g, [64, 4, 4, 128], BF16) for g in range(2)]
    Mn = [{n: sbuf(st, "%s_%d" % (n, g), [128, 4, 128], BF16) for n in ["Mab", "MabT", "Mak", "Mrb", "Mrk"]}
          for g in range(2)]
    S = [sbuf(st, "S%d" % g, [64, 4, 2, 64]) for g in range(2)]
    PQG = [sbuf(st, "PQG%d" % g, [64, 4, 256]) for g in range(2)]
    rt1 = [sbuf(st, "rt1_%d" % g, [64, 4, 2, 64]) for g in range(2)]
    yab = sbuf(st, "yab", [128, CH], BF16)
    yat = sbuf(st, "yat", [128, 4, 128], BF16)
    for g in range(2):
        P.memset('pool', S[g][:], 0.0)

    EXP = AF.Exp
    class PV:
        def __init__(self, pn):
            self.pn = pn
            self.kk = pn[:, 0:CH]
            self.v = pn[:, CH:2 * CH]
            self.low = pn[:, 2 * CH:2 * CH + 256]
            self.r = pn[:, 2 * CH + 256:3 * CH + 256]
    pv = [PV(pnb[0]), PV(pnb[1])]
    h3 = lambda ap: ap.rearrange("p (h i) -> p h i", i=64)

    def sigmoid_from(out, in_, scale=-1.0):
        P.act(out, in_, EXP, scale=scale)
        P.act(out, out, AF.Ln, bias=ones32[:, 0:1])
        P.act(out, out, EXP, scale=-1.0)

    def make_h(si, is_ctx):
        gi, so = (2, 3) if is_ctx else (0, 1)
        P.dma(xb[:], xs[si])
        P.dma(vb[:], valid[si])
        P.tt('pool', sq[:], xb[:], xb[:], ALU.mult)
        pstat = bank_f1()
        for k in range(8):
            P.mm(pstat[:, 0:256], onesb[:], sq[:, k, :], start=(k == 0), stop=(k == 7))
        yield
        P.ts('dve', rs[:], pstat[:, 0:256], 1.0 / D, ALU.mult, 1e-6, ALU.add)
        P.act(rs[:], rs[:], AF.Ln)
        P.act(rs[:], rs[:], EXP, scale=-0.5)
        P.tt('pool', xb[:], xb[:], rs[:].unsqueeze(1).to_broadcast([128, 8, 256]), ALU.mult)
        yield
        for k in range(8):
            P.ts('pool', hT[:, k, :], xb[:, k, :], gsh[:, k, gi:gi + 1], ALU.mult, gsh[:, k, so:so + 1], ALU.add)
        for a_ in (0, 192):
            P.tt('pool', hT[:, :, a_:a_ + 64], hT[:, :, a_:a_ + 64],
                 vb[:, a_:a_ + 64].unsqueeze(1).to_broadcast([128, 8, 64]), ALU.mult)
        yield
        for X in range(4):
            if is_ctx:
                off = -1 if X % 2 == 0 else 1
                msk = None
            else:
                off = [-1, 1, -64, 64][X]
                msk = [mL, mR, None, None][X]
            src = hT[:, :, 64 + off:192 + off]
            eng_ = 'pool'
            P.tt(eng_, hd[X][:], src, hT[:, :, 64:192], ALU.subtract)
            if msk is not None:
                for t_ in ((0, 64) if X == 0 else (63, 127)):
                    P.ts(eng_, hd[X][:, :, t_:t_ + 1], hT[:, :, 64 + t_:65 + t_], -1.0, ALU.mult)
        yield

    def big_mm(ncols, pvx):
        pnv = pvx.pn[:].rearrange("p (m x) -> p m x", x=4)
        for X in range(4):
            pa = bank_f1()
            pb_ = bank_f1()
            for k in range(8):
                P.mm(pa[:, 0:ncols], hT[:, k, 64:192], W[:, k, X, 0:ncols], start=(k == 0), stop=(k == 7))
            for k in range(8):
                P.mm(pb_[:, 0:ncols], hd[X][:, k, :], W[:, k, X, 0:ncols], start=(k == 0), stop=(k == 7))
            P.tt('dve', mtmp[:, 0:ncols], pb_[:, 0:ncols], mu[:, X, 0:ncols], ALU.mult)
            P.tt('dve', pnv[:, 0:ncols, X], mtmp[:, 0:ncols], pa[:, 0:ncols], ALU.add)
            yield

    def prep_common(po, pvx):
        kk_, v_ = pvx.kk, pvx.v
        P.tt('pool', T[0][:], kk_, kkrow[:], ALU.mult)
        P.tt('pool', T[1][:], T[0][:], T[0][:], ALU.mult)
        P.reduce(small[:, 0, :], h3(T[1][:]))
        P.ts('dve', small[:, 0, :], small[:, 0, :], 1e-24, ALU.max)
        P.act(small[:, 1, :], small[:, 0, :], AF.Sqrt)
        P.recip(small[:, 1, :], small[:, 1, :])
        P.tt('pool', h3(T[0][:]), h3(T[0][:]), small[:, 1, :].unsqueeze(2).to_broadcast([128, HP, 64]), ALU.mult)
        P.copy('act', po["Vb"][:], v_)
        yield

    def lora(mode, slot, pvx):
        low_ = pvx.low
        lowv = low_.rearrange("p (a d l) -> p a d l", a=2, d=2)
        if mode == 'flag':
            f0 = fls[:, slot, 1:2]
            f1 = fls[:, slot, 0:1]
            P.tt('pool', ltmp[:], lowv[:, :, 1, :], lowv[:, :, 0, :], ALU.subtract)
            P.stt(lsel[:], ltmp[:], f1, lowv[:, :, 0, :], ALU.mult, ALU.add)
            wsel = lsel[:, 0, :]
            asel = lsel[:, 1, :]
            P.act(ltmp[:, 0, :], wsel, EXP, scale=-2.0)
            P.ts('dve', ltmp[:, 0, :], ltmp[:, 0, :], 1.0, ALU.add)
            P.recip(ltmp[:, 0, :], ltmp[:, 0, :])
            P.ts('dve', ltmp[:, 0, :], ltmp[:, 0, :], 2.0, ALU.mult, -1.0, ALU.add)
            P.ts('pool', LW[:, 0:64], ltmp[:, 0, :], f0, ALU.mult)
            P.ts('pool', LW[:, 64:128], ltmp[:, 0, :], f1, ALU.mult)
            P.ts('pool', LA[:, 0:64], asel, f0, ALU.mult)
            P.ts('pool', LA[:, 64:128], asel, f1, ALU.mult)
            P.ts('pool', frow[:], ones33[:], fls[0:33, slot, 2:3], ALU.mult)
        else:
            d = slot
            cs = slice(64 * d, 64 * d + 64)
            P.memset('pool', LW[:], 0.0)
            P.memset('pool', LA[:], 0.0)
            P.act(LW[:, cs], low_[:, cs], EXP, scale=-2.0)
            P.ts('dve', LW[:, cs], LW[:, cs], 1.0, ALU.add)
            P.recip(LW[:, cs], LW[:, cs])
            P.ts('dve', LW[:, cs], LW[:, cs], 2.0, ALU.mult, -1.0, ALU.add)
            P.copy('pool', LA[:, cs], low_[:, 128 + 64 * d:128 + 64 * d + 64])
        pt = bank_f()
        P.tr(pt[:, 0:128], LW[:], ident[:])
        P.tr(pt[:, 128:256], LA[:], ident[:])
        P.copy('act', LWT[:], pt[:, 0:128])
        P.copy('act', LAT[:], pt[:, 128:256])
        fr = frow if mode == 'flag' else sel33[slot]
        pw = bank_f()
        pa = bank_f()
        P.mm(pw[:, 0:CH], LWT[:], w2s[:], start=True, stop=False, r32=True)
        P.mm(pw[:, 0:CH], fr[:], w0s[:], start=False, stop=True)
        P.mm(pa[:, 0:CH], LAT[:], a2s[:], start=True, stop=False, r32=True)
        P.mm(pa[:, 0:CH], fr[:], a0s[:], start=False, stop=True)
        return pw, pa

    def prep_dir(po, zs, gm, pw, pa, tri, readout, bwd, pvx):
        kk_, r_ = pvx.kk, pvx.r
        ld, asg, kd, bv = T[2], T[3], T[4], T[5]
        sigmoid_from(ld[:], pw[:, 0:CH])
        P.act(ld[:], ld[:], AF.Copy, scale=-EM05)
        sigmoid_from(asg[:], pa[:, 0:CH])
        yield
        P.stt(kd[:], asg[:], -1.0, karow[:], ALU.add, ALU.mult)
        P.stt(kd[:], kd[:], 1.0, kk_, ALU.add, ALU.mult)
        P.tt('pool', bv[:], T[0][:], asg[:], ALU.mult)
        pc = bank_f()
        ptot = bank_f()
        P.mm(pc[:, 0:CH], tri, ld[:], r32=True)
        P.mm(ptot[:, 0:CH], ones32[:], ld[:], r32=True)
        yield
        eexc, eninc, etot, ehat = T[6], T[7], T[8], T[3]
        P.tt('dve', eexc[:], pc[:, 0:CH], ld[:], ALU.subtract)
        P.act(eexc[:], eexc[:], EXP)
        P.act(eninc[:], pc[:, 0:CH], EXP, scale=-1.0)
        P.act(etot[:], ptot[:, 0:CH], EXP)
        if readout and not bwd:
            P.act(T[1][:], pc[:, 0:CH], EXP)
        yield
        P.tt('pool', ehat[:], etot[:], eninc[:], ALU.mult)
        for g in range(2):
            P.stt(zs[g][:, :, 0:64], h3(T[0][:, 256 * g:256 * g + 256]), -1.0, h3(eexc[:, 256 * g:256 * g + 256]),
                  ALU.mult, ALU.mult)
        P.tt('pool', po["Bt"][:], bv[:], eninc[:], ALU.mult)
        P.tt('pool', po["Kt"][:], kd[:], eninc[:], ALU.mult)
        yield
        P.tt('pool', po["Bh"][:], bv[:], ehat[:], ALU.mult)
        P.tt('pool', po["Kh"][:], kd[:], ehat[:], ALU.mult)
        if readout:
            P.tt('pool', po["Rt"][:], r_, (eexc if bwd else T[1])[:], ALU.mult)
        pg = bank_f()
        for h in range(HP):
            P.mm(pg[0:64, h:h + 1], etot[0:1, 64 * h:64 * h + 64], one11[:, :])
        P.copy('act', gm[:], pg[0:64, 0:HP])
        yield

    def chunk_group(po, zs, gm, g, mM, mMT, mR_, readout):
        bk = bank_g[g]
        Z = zs[g]
        M = Mn[g]
        ft = featT[g]
        hc = lambda h: slice(256 * g + 64 * h, 256 * g + 64 * h + 64)
        quants = ["At", "Rt", "Bt", "Kt"] if readout else ["At", None, "Bt", "Kt"]
        for pl in range(2):
            psb = bk()[:, :].bitcast(BF16)
            for hh in range(2):
                h = 2 * pl + hh
                for qi, qn in enumerate(quants):
                    if qn is None:
                        continue
                    src_ = Z[:, h, 0:64] if qn == "At" else po[qn][:, hc(h)]
                    P.tr(psb[0:64, (hh * 4 + qi) * 128:(hh * 4 + qi) * 128 + 128], src_, identb[:])
            src = psb[0:64, :].rearrange("p (h q t) -> p h q t", h=2, q=4)
            if readout:
                P.copy('act', ft[:, 2 * pl:2 * pl + 2, :, :], src)
            else:
                P.copy('act', ft[:, 2 * pl:2 * pl + 2, 0, :], src[:, :, 0, :])
                P.copy('act', ft[:, 2 * pl:2 * pl + 2, 2:4, :], src[:, :, 2:4, :])
            yield

        def fa(h, qi):
            return ft[:, h, qi, :]
        v3 = lambda b_: b_[:, :].rearrange("p (h t) -> p h t", t=128)
        bc = lambda m_: m_.unsqueeze(1).to_broadcast([128, 4, 128])
        g1, g2 = bk(), bk()
        for h in range(4):
            P.mm(g1[:, 128 * h:128 * h + 128], fa(h, 2), fa(h, 0))
            P.mm(g2[:, 128 * h:128 * h + 128], fa(h, 0), fa(h, 2))
        P.tt('dve', M["Mab"][:], v3(g1), bc(mM), ALU.mult)
        P.tt('dve', M["MabT"][:], v3(g2), bc(mMT), ALU.mult)
        yield
        g3 = bk()
        for h in range(4):
            P.mm(g3[:, 128 * h:128 * h + 128], fa(h, 3), fa(h, 0))
        P.tt('dve', M["Mak"][:], v3(g3), bc(mM), ALU.mult)
        if readout:
            g4 = bk()
            for h in range(4):
                P.mm(g4[:, 128 * h:128 * h + 128], fa(h, 2), fa(h, 1))
            P.tt('dve', M["Mrb"][:], v3(g4), bc(mR_), ALU.mult)
            yield
            g5 = bk()
            for h in range(4):
                P.mm(g5[:, 128 * h:128 * h + 128], fa(h, 3), fa(h, 1))
            P.tt('dve', M["Mrk"][:], v3(g5), bc(mR_), ALU.mult)
        yield
        pe_ = bk()
        for h in range(4):
            P.mm(pe_[:, 64 * h:64 * h + 64], M["Mak"][:, h, :], po["Vb"][:, hc(h)])
        P.copy('act', Z[:, :, 64:128], pe_[:, 0:256].rearrange("p (h i) -> p h i", i=64))
        yield
        N_, NT_ = M["Mab"], M["MabT"]
        for lvl in range(7):
            pz = bk()
            for h in range(4):
                P.mm(pz[:, 128 * h:128 * h + 128], N_[:, h, :], Z[:, h, :])
            if lvl < 6:
                pn2 = bk()
                for h in range(4):
                    P.mm(pn2[:, 128 * h:128 * h + 128], NT_[:, h, :], N_[:, h, :])
            P.tt('dve', Z[:], v3(pz), Z[:], ALU.add)
            if lvl < 5:
                yield
                pn2t = bk()
                for h in range(4):
                    P.mm(pn2t[:, 128 * h:128 * h + 128], N_[:, h, :], NT_[:, h, :])
            if lvl < 6:
                P.copy('act', N_[:], v3(pn2))
            if lvl < 5:
                P.copy('act', NT_[:], v3(pn2t))
            yield
        pp = bk()
        for h in range(4):
            P.mm(pp[0:64, 64 * h:64 * h + 64], Z[:, h, 0:64], po["Bh"][:, hc(h)])
        for h in range(4):
            P.stt(PQG[g][:, h, 0:64], ident[0:64, 0:64], gm[:, 4 * g + h:4 * g + h + 1], pp[0:64, 64 * h:64 * h + 64],
                  ALU.mult, ALU.add)
        yield
        pq = bk()
        for h in range(4):
            P.mm(pq[0:64, 64 * h:64 * h + 64], po["Bh"][:, hc(h)], Z[:, h, 64:128], start=True, stop=False)
            P.mm(pq[0:64, 64 * h:64 * h + 64], po["Kh"][:, hc(h)], po["Vb"][:, hc(h)], start=False, stop=True)
        P.copy('act', PQG[g][:, :, 64:128], pq[0:64, 0:256].rearrange("p (h i) -> p h i", i=64))
        yield
        if readout:
            pgt = bk()
            for h in range(4):
                P.mm(pgt[0:64, 128 * h:128 * h + 128], Z[:, h, 0:64], M["Mrb"][:, h, :], start=True, stop=False)
                P.mm(pgt[0:64, 128 * h:128 * h + 128], po["Rt"][:, hc(h)], identb[:], start=False, stop=True)
            P.copy('act', PQG[g][:, :, 128:256], pgt[0:64, :].rearrange("p (h t) -> p h t", t=128))
            yield

    def yloc_mm(py, po, zs, g, with_state, sidx):
        Z = zs[g]
        M = Mn[g]
        for h in range(4):
            c = slice(256 * g + 64 * h, 256 * g + 64 * h + 64)
            o = py[:, 64 * h:64 * h + 64]
            P.mm(o, M["Mrb"][:, h, :], Z[:, h, 64:128], start=True, stop=False)
            P.mm(o, M["Mrk"][:, h, :], po["Vb"][:, c], start=False, stop=not with_state)
            if with_state:
                P.mm(o, PQG[g][:, h, 128:256], S[g][:, h, sidx, :], start=False, stop=True)

    def front1_flag(si):
        yield from make_h(si, si < 4)
        yield from big_mm(320, pv[si % 2])

    def front2_flag(si):
        par = si % 2
        pvx = pv[par]
        po, zs, gm = PO[par], Zall[par], gam[par]
        yield from prep_common(po, pvx)
        f1 = fls[:, si, 0:1]
        P.stt(Msel[par][:], D1[:], f1, SU[:], ALU.mult, ALU.add)
        P.tt('pool', MselT[par][:], SU[:], SL[:], ALU.add)
        P.tt('pool', MselT[par][:], MselT[par][:], Msel[par][:], ALU.subtract)
        P.stt(Trisel[:], D2[:], f1, Tri[:], ALU.mult, ALU.add)
        P.ts('pool', fbm[par][:, :, 0, :], ones64[:], fls[0:64, si, 1:2], ALU.mult)
        P.ts('pool', fbm[par][:, :, 1, :], ones64[:], fls[0:64, si, 0:1], ALU.mult)
        pw, pa = lora('flag', si, pvx)
        yield
        yield from prep_dir(po, zs, gm, pw, pa, Trisel[:], False, False, pvx)

    def chunk_flag(si, g):
        par = si % 2
        po, zs, gm = PO[par], Zall[par], gam[par]
        yield from chunk_group(po, zs, gm, g, Msel[par][:], MselT[par][:], None, False)
        ps_ = bank_g[g]()
        for h in range(4):
            P.mm(ps_[0:64, 128 * h:128 * h + 128], PQG[g][:, h, 0:64],
                 S[g][:, h, :, :].rearrange("p a i -> p (a i)"))
        psv = ps_[0:64, :].rearrange("p (h a i) -> p h a i", a=2, i=64)
        P.tt('dve', rt1[g][:], psv, PQG[g][:, :, 64:128].unsqueeze(2).to_broadcast([64, 4, 2, 64]), ALU.add)
        P.tt('dve', rt1[g][:], rt1[g][:], S[g][:], ALU.subtract)
        P.tt('dve', rt1[g][:], rt1[g][:], fbm[par][:], ALU.mult)
        P.tt('dve', S[g][:], S[g][:], rt1[g][:], ALU.add)
        yield

    nfl = LIM['nflag']
    for si in range(nfl + 2):
        tasks = []
        if 0 <= si - 2 < nfl:
            tasks += [chunk_flag(si - 2, 0), chunk_flag(si - 2, 1)]
        if 0 <= si - 1 < nfl:
            tasks.append((front2_flag(si - 1), PACE['f2']))
        if si < nfl:
            tasks.append((front1_flag(si), PACE['f1']))
        run_tasks(tasks)
    if hp == 0:
        dbg("S0", S[0][:], [64, 4, 2, 64])
        dbg("S1", S[1][:], [64, 4, 2, 64])

    def own_A1(i):
        si = NFLAG + i
        yp = ypart[i % 2]
        pvx = pv[i % 2]
        yield from make_h(si, False)
        yield from big_mm(NCOLT, pvx)
        if hp == 0 and i == 0:
            dbg("pn", pvx.pn[:], [128, 4 * NCOLT])
        pga = bank_f1()
        for k in range(8):
            P.mm(pga[:, 0:CH], hT[:, k, 64:192], Wg[:, k, :], start=(k == 0), stop=(k == 7))
        sigmoid_from(yp[:, 2, :], pga[:, 0:CH])
        P.tt('dve', yp[:, 2, :], yp[:, 2, :], pga[:, 0:CH], ALU.mult)
        yield

    def own_A2(i):
        yp = ypart[i % 2]
        pvx = pv[i % 2]
        yield from prep_common(PO[0], pvx)
        pw, pa = lora('own', 0, pvx)
        yield
        yield from prep_dir(PO[0], Zall[0], gam[0], pw, pa, Tri[:], True, False, pvx)
        P.tt('pool', T[1][:], pvx.r, T[4][:], ALU.mult)
        P.tt('pool', T[1][:], T[1][:], rkrow[:], ALU.mult)
        P.reduce(small[:, 2, :], h3(T[1][:]))
        P.tt('pool', h3(yp[:, 1, :]), h3(pvx.v), small[:, 2, :].unsqueeze(2).to_broadcast([128, HP, 64]), ALU.mult)
        yield

    def own_B(i):
        pvx = pv[i % 2]
        P.copy('pool', PO[1]["Vb"][:], PO[0]["Vb"][:])
        pw, pa = lora('own', 1, pvx)
        yield
        yield from prep_dir(PO[1], Zall[1], gam[1], pw, pa, TriT[:], True, True, pvx)

    def own_C(i, d, g):
        yp = ypart[i % 2]
        po, zs, gm = PO[d], Zall[d], gam[d]
        if d == 0:
            yield from chunk_group(po, zs, gm, g, SU[:], SL[:], Tri[:], True)
            py = bank_g[g]()
            yloc_mm(py, po, zs, g, True, 0)
            P.copy('act', yp[:, 0, 256 * g:256 * g + 256], py[:, 0:256])
            yield
            ps_ = bank_g[g]()
            for h in range(4):
                P.mm(ps_[0:64, 64 * h:64 * h + 64], PQG[g][:, h, 0:64], S[g][:, h, 0, :])
            P.tt('dve', S[g][:, :, 0, :], ps_[0:64, 0:256].rearrange("p (h i) -> p h i", i=64),
                 PQG[g][:, :, 64:128], ALU.add)
            yield
        else:
            yield from chunk_group(po, zs, gm, g, SL[:], SU[:], SL[:], True)
            py = bank_g[g]()
            yloc_mm(py, po, zs, g, False, 1)
            P.tt('dve', yp[:, 0, 256 * g:256 * g + 256], yp[:, 0, 256 * g:256 * g + 256], py[:, 0:256], ALU.add)
            P.dma(scr_pqg[i, g], PQG[g][:], writes=["scr_pqg%d_%d" % (i, g)])
            yield

    nown = LIM['nown']
    if nown > 0:
        run_tasks([own_A1(0)])
        run_tasks([own_A2(0)])
    for i in range(nown):
        run_tasks([own_C(i, 0, 0), own_C(i, 0, 1), (own_B(i), PACE['f2']),
                   (own_A1(i + 1) if i + 1 < nown else None, PACE['f1'])])
        run_tasks([own_C(i, 1, 0), own_C(i, 1, 1), (own_A2(i + 1) if i + 1 < nown else None, PACE['f2'])])
        P.dma(scr_y[i], ypart[i % 2][:], writes=["scr_y%d" % i])

    lnxw, lnxb = T[2], T[3]
    if LIM['passB'] and nown > 0:
        P.dma(lnxw[:], G['lnxw_d'][hp])
        P.dma(lnxb[:], G['lnxb_d'][hp])
    for i in (range(nown - 1, -1, -1) if LIM['passB'] else []):
        yld = ypart[i % 2]
        P.dma(yld[:], scr_y[i], reads=["scr_y%d" % i])
        for g in range(2):
            P.dma(PQG[g][:], scr_pqg[i, g], reads=["scr_pqg%d_%d" % (i, g)])
            py = bank_g[g]()
            for h in range(4):
                P.mm(py[:, 64 * h:64 * h + 64], PQG[g][:, h, 128:256], S[g][:, h, 1, :])
            P.tt('dve', yld[:, 0, 256 * g:256 * g + 256], yld[:, 0, 256 * g:256 * g + 256], py[:, 0:256], ALU.add)
            ps_ = bank_g[g]()
            for h in range(4):
                P.mm(ps_[0:64, 64 * h:64 * h + 64], PQG[g][:, h, 0:64], S[g][:, h, 1, :])
            P.tt('dve', S[g][:, :, 1, :], ps_[0:64, 0:256].rearrange("p (h i) -> p h i", i=64),
                 PQG[g][:, :, 64:128], ALU.add)
        y3 = h3(yld[:, 0, :])
        if hp == 0 and i == 0:
            dbg("ysum", yld[:, 0, :], [128, CH])
        P.reduce(small[:, 0, :], y3)
        P.ts('dve', small[:, 0, :], small[:, 0, :], 1.0 / 64, ALU.mult)
        P.tt('dve', h3(T[0][:]), y3, small[:, 0, :].unsqueeze(2).to_broadcast([128, HP, 64]), ALU.subtract)
        P.tt('pool', T[1][:], T[0][:], T[0][:], ALU.mult)
        P.reduce(small[:, 1, :], h3(T[1][:]))
        P.ts('dve', small[:, 1, :], small[:, 1, :], 1.0 / 64, ALU.mult, 64e-5, ALU.add)
        P.act(small[:, 1, :], small[:, 1, :], AF.Sqrt)
        P.recip(small[:, 1, :], small[:, 1, :])
        P.tt('dve', h3(T[0][:]), h3(T[0][:]), small[:, 1, :].unsqueeze(2).to_broadcast([128, HP, 64]), ALU.mult)
        P.tt('pool', T[0][:], T[0][:], lnxw[:], ALU.mult)
        P.tt('pool', T[0][:], T[0][:], lnxb[:], ALU.add)
        P.tt('dve', T[0][:], T[0][:], yld[:, 1, :], ALU.add)
        P.tt('dve', yab[:], T[0][:], yld[:, 2, :], ALU.mult)
        if hp == 0 and i == 0:
            dbg("yab0", yab[:], [128, CH], BF16)
        psb = bank_g[0]()[:, :].bitcast(BF16)
        for cc in range(4):
            P.tr(psb[:, 128 * cc:128 * cc + 128], yab[:, 128 * cc:128 * cc + 128], identb[:])
        P.copy('act', yat[:], psb[:, 0:512].rearrange("p (c t) -> p c t", t=128))
        P.dma(scr_ya[:, 4 * hp:4 * hp + 4, 128 * i:128 * i + 128], yat[:], writes=["scr_ya"])


def phase2(nc, P, st, sbuf, bank, psb, G, dbg):
    gsh, gateB, onesb, ones32 = G['gsh'], G['gateB'], G['onesb'], G['ones32']
    x2T, edge, x_own, wblk, y_out, scr_ya = G['x2T'], G['edge'], G['x_own'], G['wblk'], G['y_out'], G['scr_ya']
    TB = 1024
    TH = TB + 32
    SEG = TH // 3
    EXP = AF.Exp

    cw = sbuf(st, "cw", [128, 8, 31])
    cv = sbuf(st, "cv", [128, 3, 8])
    fnws = sbuf(st, "fnws", [128, D])
    edg = sbuf(st, "edg", [128, 2])
    P.dma(cw[:], G['cwT'])
    P.dma(cv[:], G['cvec'])
    P.dma(fnws[:], G['fnw'])
    P.dma(edg[:], edge)

    x2k = [sbuf(st, "x2k%d" % i, [128, TH]) for i in range(2)]
    sq2 = sbuf(st, "sq2", [128, TH], BF16)
    rs2 = sbuf(st, "rs2", [128, TH])
    h2 = sbuf(st, "h2", [128, 8, TH], BF16)
    zt = sbuf(st, "zt", [128, 2, TH])
    sgt = sbuf(st, "sgt", [128, SEG])
    acc = sbuf(st, "acc", [128, 8, TB])
    ybT = sbuf(st, "ybT", [128, 8, TB], BF16)
    mgT = sbuf(st, "mgT", [128, 8, TB], BF16)
    yaT = sbuf(st, "yaT", [128, 8, TB], BF16)
    wst = sbuf(st, "wst", [128, 8, 512])
    wbf = [sbuf(st, "wbf%d" % i, [128, 8, 512], BF16) for i in range(3)]
    mean = sbuf(st, "mean", [128, 512])
    rstd = sbuf(st, "rstd", [128, 512])
    tmpA = sbuf(st, "tmpA", [128, 512])
    tmpB = sbuf(st, "tmpB", [128, 512])
    e1 = sbuf(st, "e1", [128, 512])
    e2 = sbuf(st, "e2", [128, 512])
    e3 = sbuf(st, "e3", [128, 512])
    xo = sbuf(st, "xo", [128, D])
    xnw = sbuf(st, "xnw", [128, D])
    junk = sbuf(st, "junk", [128, D])
    ssq = sbuf(st, "ssq", [128, 2])
    wcnt = [0]

    seq1 = [0, 1, 2, 3, 4, 5, 6, 10, 7, 11, 8, 12, 9, 13, 14, 15]
    seq = seq1 + seq1
    loaded = [0]
    usep = [0]

    def _issue(p):
        d_ = wbf[p % 3]
        P.dma(wst[:], wblk[seq[p]])
        P.copy('pool', d_[:], wst[:])

    def load_block(bi):
        p = usep[0]
        usep[0] += 1
        assert seq[p] == bi, (p, seq[p], bi)
        while loaded[0] <= min(p + 1, len(seq) - 1):
            _issue(loaded[0])
            loaded[0] += 1
        return wbf[p % 3]

    def sigmoid_from(out, in_):
        P.act(out, in_, EXP, scale=-1.0)
        P.act(out, out, AF.Ln, bias=ones32[:, 0:1])
        P.act(out, out, EXP, scale=-1.0)

    for hb in range(2):
        t0 = TB * hb
        P.dma(yaT[:], scr_ya[:, :, t0:t0 + TB], reads=["scr_ya"])
        pst = [bank(), bank(), bank()]
        for k in range(8):
            xk = x2k[k % 2]
            P.dma(xk[:], x2T[:, k, t0:t0 + TH])
            P.act(sq2[:], xk[:], AF.Square)
            for s in range(3):
                P.mm(pst[s][:, 0:SEG], onesb[:], sq2[:, SEG * s:SEG * s + SEG], start=(k == 0), stop=(k == 7))
        for s in range(3):
            P.ts('dve', rs2[:, SEG * s:SEG * s + SEG], pst[s][:, 0:SEG], 1.0 / D, ALU.mult, 1e-6, ALU.add)
        P.act(rs2[:], rs2[:], AF.Ln)
        P.act(rs2[:], rs2[:], EXP, scale=-0.5)
        for k in range(8):
            xk = x2k[k % 2]
            P.dma(xk[:], x2T[:, k, t0:t0 + TH])
            P.tt('dve', xk[:], xk[:], rs2[:], ALU.mult)
            P.ts('pool', h2[:, k, :], xk[:], gsh[:, k, 0:1], ALU.mult, gsh[:, k, 1:2], ALU.add)

        for j in range(4):
            wb = load_block(j)
            for cc in range(2):
                c = 2 * j + cc
                for s in range(3):
                    pu, pg = bank(), bank()
                    for k in range(8):
                        P.mm(pu[:, 0:SEG], wb[:, k, 128 * cc:128 * cc + 128], h2[:, k, SEG * s:SEG * s + SEG],
                             start=(k == 0), stop=(k == 7))
                    for k in range(8):
                        P.mm(pg[:, 0:SEG], wb[:, k, 256 + 128 * cc:256 + 128 * cc + 128],
                             h2[:, k, SEG * s:SEG * s + SEG], start=(k == 0), stop=(k == 7))
                    sigmoid_from(sgt[:], pg[:, 0:SEG])
                    P.tt('dve', zt[:, cc, SEG * s:SEG * s + SEG], pu[:, 0:SEG], sgt[:], ALU.mult)
                if hb == 0:
                    P.ts('pool', zt[:, cc, 0:16], zt[:, cc, 0:16], edg[:, 0:1], ALU.mult)
                else:
                    P.ts('pool', zt[:, cc, TH - 16:TH], zt[:, cc, TH - 16:TH], edg[:, 1:2], ALU.mult)
                P.ts('dve', acc[:, c, :], zt[:, cc, 1:1 + TB], cw[:, c, 0:1], ALU.mult, cv[:, 0, c:c + 1], ALU.add)
                for k in range(1, 31):
                    P.stt(acc[:, c, :], zt[:, cc, k + 1:k + 1 + TB], cw[:, c, k:k + 1], acc[:, c, :],
                          ALU.mult, ALU.add)
        if hb == 0:
            dbg("acc", acc[:], [128, 8, TB])
        wgb = [load_block(4), load_block(5)]
        for s in range(2):
            sl = slice(512 * s, 512 * s + 512)
            p1, p2 = bank(), bank()
            for c in range(8):
                P.mm(p1[:, :], ones32[:], acc[:, c, sl], start=(c == 0), stop=(c == 7))
            for c in range(8):
                P.act(e2[:], acc[:, c, sl], AF.Square)
                P.mm(p2[:, :], ones32[:], e2[:], start=(c == 0), stop=(c == 7))
            P.ts('dve', mean[:], p1[:, :], 1.0 / D, ALU.mult)
            P.tt('dve', e3[:], mean[:], mean[:], ALU.mult)
            P.stt(rstd[:], p2[:, :], 1.0 / D, e3[:], ALU.mult, ALU.subtract)
            P.ts('dve', rstd[:], rstd[:], 1e-5, ALU.add)
            P.act(rstd[:], rstd[:], AF.Sqrt)
            P.recip(rstd[:], rstd[:])
            for c in range(8):
                pg = bank()
                for k in range(8):
                    P.mm(pg[:, :], wgb[c // 4][:, k, 128 * (c % 4):128 * (c % 4) + 128],
                         h2[:, k, 16 + 512 * s:16 + 512 * s + 512], start=(k == 0), stop=(k == 7))
                sigmoid_from(e1[:], pg[:, :])
                P.tt('dve', e1[:], pg[:, :], e1[:], ALU.mult)
                P.tt('pool', tmpA[:], acc[:, c, sl], mean[:], ALU.subtract)
                P.tt('pool', tmpA[:], tmpA[:], rstd[:], ALU.mult)
                P.ts('pool', tmpA[:], tmpA[:], cv[:, 1, c:c + 1], ALU.mult, cv[:, 2, c:c + 1], ALU.add)
                sigmoid_from(tmpB[:], tmpA[:])
                P.tt('pool', tmpA[:], tmpA[:], tmpB[:], ALU.mult)
                P.tt('pool', ybT[:, c, sl], tmpA[:], e1[:], ALU.mult)
        if hb == 0:
            dbg("ybT", ybT[:], [128, 8, TB], BF16)
        for rnd in range(2):
            for jb in range(2):
                wm_b = load_block((6 if rnd == 0 else 8) + jb)
                wp_b = load_block((10 if rnd == 0 else 12) + jb)
                for mm_ in range(4):
                    m = 4 * jb + mm_
                    for s in range(2):
                        sl = slice(512 * s, 512 * s + 512)
                        pm_, pp_ = bank(), bank()
                        for k in range(8):
                            P.mm(pm_[:, :], wm_b[:, k, 128 * mm_:128 * mm_ + 128],
                                 h2[:, k, 16 + 512 * s:16 + 512 * s + 512], start=(k == 0), stop=(k == 7))
                        for k in range(8):
                            rhs = (yaT if rnd == 0 else ybT)[:, k, sl]
                            P.mm(pp_[:, :], wp_b[:, k, 128 * mm_:128 * mm_ + 128], rhs, start=(k == 0), stop=(k == 7))
                        sigmoid_from(e1[:], pm_[:, :])
                        if rnd == 0:
                            P.tt('dve', mgT[:, m, sl], pp_[:, :], e1[:], ALU.mult)
                        else:
                            P.tt('dve', e2[:], pp_[:, :], e1[:], ALU.mult)
                            P.tt('pool', mgT[:, m, sl], mgT[:, m, sl], e2[:], ALU.add)
        wo = [load_block(14), load_block(15)]
        for tt_ in range(8):
            ti = 8 * hb + tt_
            P.dma(xo[:], x_own[ti])
            for n in range(2):
                po_ = bank()
                for k in range(8):
                    P.mm(po_[:, :], mgT[:, k, 128 * tt_:128 * tt_ + 128], wo[n][:, k, :], start=(k == 0), stop=(k == 7))
                sl = slice(512 * n, 512 * n + 512)
                P.tt('dve', xnw[:, sl], po_[:, :], gateB[:, sl], ALU.mult)
                P.tt('pool', xnw[:, sl], xnw[:, sl], xo[:, sl], ALU.add)
            P.act(junk[:], xnw[:], AF.Square, accum_out=ssq[:, 0:1])
            P.ts('dve', ssq[:, 1:2], ssq[:, 0:1], 1.0 / D, ALU.mult, 1e-6, ALU.add)
            P.act(ssq[:, 1:2], ssq[:, 1:2], AF.Sqrt)
            P.recip(ssq[:, 1:2], ssq[:, 1:2])
            P.stt(junk[:], xnw[:], ssq[:, 1:2], fnws[:], ALU.mult, ALU.mult)
            P.dma(y_out[ti], junk[:])


def _bc(v, n=128):
    v = np.asarray(v, np.float32)
    return np.ascontiguousarray(np.broadcast_to(v[None], (n,) + v.shape))


def _fm(a):
    t = a.shape[0]
    return np.ascontiguousarray(a.T.reshape(8, 128, t).transpose(1, 0, 2))


def prep_inputs(inp):
    f32 = np.float32
    x, c, ctx, c_ctx = [np.asarray(inp[k], f32) for k in ("x", "c", "ctx", "c_ctx")]
    w_in = np.asarray(inp["w_in"], f32)[0]
    mu_shift = np.asarray(inp["mu_shift"], f32)[0]
    w0, w2, a0, a2 = [np.asarray(inp[k], f32)[0] for k in ("w0", "w2", "a0", "a2")]
    k_k, k_a = np.asarray(inp["k_k"], f32)[0], np.asarray(inp["k_a"], f32)[0]
    r_k = np.asarray(inp["r_k"], f32)[0].reshape(-1)
    lnx_w, lnx_b = np.asarray(inp["lnx_w"], f32)[0], np.asarray(inp["lnx_b"], f32)[0]
    conv_w, conv_b = np.asarray(inp["conv_w"], f32)[0], np.asarray(inp["conv_b"], f32)[0]
    cln_w, cln_b = np.asarray(inp["cln_w"], f32)[0], np.asarray(inp["cln_b"], f32)[0]
    w_mod, b_mod = np.asarray(inp["w_mod"], f32)[0], np.asarray(inp["b_mod"], f32)[0]
    norm_w = np.asarray(inp["norm_w"], f32)[0]
    wpa, wpb, wout = [np.asarray(inp[k], f32)[0] for k in ("w_proj_a", "w_proj_b", "w_out")]
    fnw = np.asarray(inp["final_norm_w"], f32)

    def wk(cols):
        return np.ascontiguousarray(w_in[:, cols].reshape(8, 128, -1).transpose(1, 0, 2))

    shared = {}
    shared["w_mod"] = np.ascontiguousarray(w_mod.reshape(8, 128, 3072).transpose(1, 0, 2))
    shared["b_modT"] = np.ascontiguousarray(b_mod.reshape(24, 128).T)
    shared["bgate"] = _bc(b_mod[2048:3072])
    shared["norm_wT"] = np.ascontiguousarray(norm_w.reshape(8, 128).T)
    wrk = np.zeros((NPASS, 128, 8, 4, NCOLT), f32)
    murk = np.zeros((NPASS, 128, 4, NCOLT), f32)
    wga = np.zeros((NPASS, 128, 8, CH), f32)
    rows = {n: np.zeros((NPASS, 128, CH), f32) for n in ("kkrow", "karow", "rkrow", "lnxw", "lnxb", "w2s", "a2s")}
    w0s = np.zeros((NPASS, 33, CH), f32)
    a0s = np.zeros((NPASS, 33, CH), f32)
    for hp in range(NPASS):
        c0 = CH * hp
        for X in range(4):
            m128 = 4 * np.arange(128) + X
            m64 = 4 * np.arange(64) + X
            cols = np.concatenate([1024 + c0 + m128, 2048 + c0 + m128, 3072 + m64, c0 + m128])
            wrk[hp, :, :, X, :] = wk(cols)
            murk[hp, :, X, :] = mu_shift[cols][None, :]
        wga[hp] = wk(3328 + c0 + np.arange(CH))
        sl = slice(c0, c0 + CH)
        rows["kkrow"][hp] = k_k[sl][None]
        rows["karow"][hp] = k_a[sl][None]
        rows["rkrow"][hp] = r_k[sl][None]
        rows["lnxw"][hp] = lnx_w[sl][None]
        rows["lnxb"][hp] = lnx_b[sl][None]
        rows["w2s"][hp] = w2[:, :, sl].reshape(128, CH)
        rows["a2s"][hp] = a2[:, :, sl].reshape(128, CH)
        w0s[hp, 0], w0s[hp, 32] = w0[0, sl], w0[1, sl]
        a0s[hp, 0], a0s[hp, 32] = a0[0, sl], a0[1, sl]
    shared.update(wrk=wrk, mu_rk=murk, wga=wga, w0s=w0s, a0s=a0s, **rows)
    blocks = []
    for j in range(4):
        blocks.append(wk(np.concatenate([4352 + 256 * j + np.arange(256), 5376 + 256 * j + np.arange(256)])))
    for base in (6400, 7424, 8448):
        for j in range(2):
            blocks.append(wk(base + 512 * j + np.arange(512)))
    for wmat in (wpa, wpb, wout):
        for j in range(2):
            blocks.append(np.ascontiguousarray(wmat[:, 512 * j:512 * j + 512].reshape(8, 128, 512).transpose(1, 0, 2)))
    shared["wblk"] = np.stack(blocks)
    shared["cwT"] = np.ascontiguousarray(conv_w.T.reshape(8, 128, 31).transpose(1, 0, 2))
    shared["cvec"] = np.ascontiguousarray(np.stack([v.reshape(8, 128).T for v in (conv_b, cln_w, cln_b)], axis=1))
    shared["fnw"] = _bc(fnw)

    def slot(seq, n):
        L = seq.shape[0]
        lo, hi = 128 * n - 64, 128 * n + 192
        buf = np.zeros((256, D), f32)
        val = np.zeros((256,), f32)
        a, b_ = max(lo, 0), min(hi, L)
        buf[a - lo:b_ - lo] = seq[a:b_]
        val[a - lo:b_ - lo] = 1.0
        return _fm(buf), val

    in_maps = []
    for core in range(8):
        b, q = core // 4, core % 4
        lat, cx = x[b], ctx[b]
        sl_list = [(cx, 0, 0), (cx, 1, 0), (cx, 1, 1), (cx, 0, 1)]
        sl_list += [(lat, n, 0) for n in range(16 * q)]
        sl_list += [(lat, n, 1) for n in range(63, 16 * q + 15, -1)]
        assert len(sl_list) == NFLAG
        sl_list += [(lat, 16 * q + i, None) for i in range(NOWN)]
        xs_ = np.zeros((NSLOT, 128, 8, 256), f32)
        va_ = np.zeros((NSLOT, 128, 256), f32)
        fl_ = np.zeros((128, NFLAG, 3), f32)
        for s, (seq, n, f) in enumerate(sl_list):
            xs_[s], v = slot(seq, n)
            va_[s] = v[None, :]
            if f is not None:
                fl_[:, s, 0] = f
                fl_[:, s, 1] = 1 - f
                fl_[0, s, 2] = 1 - f
                fl_[32, s, 2] = f
        lo = 2048 * q - 16
        buf = np.zeros((2080, D), f32)
        a, b_ = max(lo, 0), min(lo + 2080, SEQ)
        buf[a - lo:b_ - lo] = lat[a:b_]
        m = dict(shared)
        m.update(xs=xs_, valid=va_, fl=fl_, x2T=_fm(buf),
                 edge=_bc(np.array([1.0 if q > 0 else 0.0, 1.0 if q < 3 else 0.0], f32)),
                 x_own=np.ascontiguousarray(lat[2048 * q:2048 * q + 2048].reshape(NOWN, 128, D)),
                 c2=np.ascontiguousarray(np.stack([c[b].reshape(8, 128).T, c_ctx.reshape(8, 128).T], axis=2)))
        in_maps.append(m)
    return in_maps


_NC_CACHE = {}


def kernel(**inputs):
    if "nc" not in _NC_CACHE:
        _NC_CACHE["nc"] = build()[0]
    nc = _NC_CACHE["nc"]
    in_maps = prep_inputs(inputs)
    res = run_bass_kernel_spmd(nc, in_maps, core_ids=list(range(8)))
    out = np.zeros((2, SEQ, D), np.float32)
    for core in range(8):
        b, q = core // 4, core % 4
        out[b, 2048 * q:2048 * q + 2048] = np.asarray(res.results[core]["y_out"]).reshape(2048, D)
    return out
```
